# Optimizing a Trainium2 kernel written in Bass

```python
import jax, jax.numpy as jnp
from jax import lax
import numpy as np

D_MODEL = 1024
BATCH = 8
SEQ = 2048
DEPTH = 1
DEC_BATCH = 128
DEC_SEQ = 8
PAST_LEN = 16384
PAGE_SIZE = 128

N_META = 16
MIX_WIDTH = D_MODEL
A_HEADS = 8
A_HEAD_DIM = 64
A_WIDTH = A_HEADS * A_HEAD_DIM
B_HEADS = 8
B_HEAD_DIM = 64
B_WIDTH = B_HEADS * B_HEAD_DIM
IN_COLS = 2 * A_WIDTH + 3 * B_WIDTH
CONV_A_W = 31
CONV_B_W = 3
CONV_F_W = 3
D_FF = 2816
EPS = 1e-6

kernel_name = "hymba_conformer_shortconv_convffn_step"


def _rmsnorm(x, g):
    xf = x.astype(jnp.float32)
    r = lax.rsqrt(jnp.mean(xf * xf, axis=-1, keepdims=True) + EPS)
    return (xf * r).astype(x.dtype) * g


def _head_layernorm(u, g, b):
    n, t, c = u.shape
    uf = u.astype(jnp.float32).reshape(n, t, A_HEADS, A_HEAD_DIM)
    mu = jnp.mean(uf, axis=-1, keepdims=True)
    var = jnp.mean(jnp.square(uf - mu), axis=-1, keepdims=True)
    un = ((uf - mu) * lax.rsqrt(var + EPS)).reshape(n, t, c).astype(u.dtype)
    return un * g + b


def _causal_dwconv(x, buf, w):
    xe = jnp.concatenate([buf.astype(x.dtype), x], axis=1)
    c = x.shape[-1]
    y = lax.conv_general_dilated(
        xe, w[:, None, :].astype(x.dtype), window_strides=(1,), padding='VALID',
        dimension_numbers=('NWC', 'WIO', 'NWC'), feature_group_count=c)
    return y, xe[:, -(w.shape[0] - 1):, :]


def _layer(x, buf_a, buf_b, buf_f, norm_mix_g, w_in, w_conv_a, b_conv_a, gn_a_g, gn_a_b,
           w_conv_b, beta_a, beta_b, w_out, norm_ffn_g, w_up, w_conv_f, w_down):
    h = _rmsnorm(x, norm_mix_g)
    proj = jnp.einsum('ntd,dc->ntc', h, w_in)
    a_val, a_gate, b_gate, c_gate, b_in = jnp.split(
        proj, [A_WIDTH, 2 * A_WIDTH, 2 * A_WIDTH + B_WIDTH, 2 * A_WIDTH + 2 * B_WIDTH], axis=-1)
    u = a_val * jax.nn.sigmoid(a_gate)
    ua, nbuf_a = _causal_dwconv(u, buf_a, w_conv_a)
    ya = jax.nn.silu(_head_layernorm(ua + b_conv_a, gn_a_g, gn_a_b)) * beta_a
    z = c_gate * b_in
    zb, nbuf_b = _causal_dwconv(z, buf_b, w_conv_b)
    yb = b_gate * zb * beta_b
    x = x + jnp.einsum('ntc,cd->ntd', jnp.concatenate([ya, yb], axis=-1), w_out)
    h = _rmsnorm(x, norm_ffn_g)
    up = jnp.einsum('ntd,df->ntf', h, w_up)
    upc, nbuf_f = _causal_dwconv(up, buf_f, w_conv_f)
    gate, val = jnp.split(upc, 2, axis=-1)
    x = x + jnp.einsum('ntf,fd->ntd', jax.nn.silu(gate) * val, w_down)
    return x, nbuf_a, nbuf_b, nbuf_f


def setup_inputs(seed: int = 0) -> dict:
    key = jax.random.key(seed)
    ks = jax.random.split(key, 24)
    f32 = jnp.float32
    nrm = lambda k, s, sc: jax.random.normal(k, s, f32) * sc
    return {
        "x_prompt": nrm(ks[0], (BATCH, SEQ, D_MODEL), 1.0),
        "x_sample": nrm(ks[1], (DEC_BATCH, DEC_SEQ, D_MODEL), 1.0),
        "state_conv_a": nrm(ks[2], (DEPTH, DEC_BATCH, CONV_A_W - 1, A_WIDTH), 0.5),
        "state_conv_b": nrm(ks[3], (DEPTH, DEC_BATCH, CONV_B_W - 1, B_WIDTH), 0.5),
        "state_conv_ffn": nrm(ks[4], (DEPTH, DEC_BATCH, CONV_F_W - 1, 2 * D_FF), 1.0),
        "meta_tokens": nrm(ks[5], (N_META, D_MODEL), 1.0),
        "norm_mix_g": 1.0 + nrm(ks[6], (DEPTH, D_MODEL), 0.02),
        "w_in": nrm(ks[7], (DEPTH, D_MODEL, IN_COLS), D_MODEL ** -0.5),
        "w_conv_a": nrm(ks[8], (DEPTH, CONV_A_W, A_WIDTH), CONV_A_W ** -0.5),
        "b_conv_a": nrm(ks[9], (DEPTH, A_WIDTH), 0.02),
        "gn_a_g": 1.0 + nrm(ks[10], (DEPTH, A_WIDTH), 0.02),
        "gn_a_b": nrm(ks[11], (DEPTH, A_WIDTH), 0.02),
        "w_conv_b": nrm(ks[12], (DEPTH, CONV_B_W, B_WIDTH), CONV_B_W ** -0.5),
        "beta_a": 1.0 + nrm(ks[13], (DEPTH, A_WIDTH), 0.02),
        "beta_b": 1.0 + nrm(ks[14], (DEPTH, B_WIDTH), 0.02),
        "w_out": nrm(ks[15], (DEPTH, MIX_WIDTH, D_MODEL), MIX_WIDTH ** -0.5),
        "norm_ffn_g": 1.0 + nrm(ks[16], (DEPTH, D_MODEL), 0.02),
        "w_up": nrm(ks[17], (DEPTH, D_MODEL, 2 * D_FF), D_MODEL ** -0.5),
        "w_conv_f": nrm(ks[18], (DEPTH, CONV_F_W, 2 * D_FF), CONV_F_W ** -0.5),
        "w_down": nrm(ks[19], (DEPTH, D_FF, D_MODEL), D_FF ** -0.5),
        "norm_final_g": 1.0 + nrm(ks[20], (D_MODEL,), 0.02),
    }


def reference(x_prompt, x_sample, state_conv_a, state_conv_b, state_conv_ffn, meta_tokens,
              norm_mix_g, w_in, w_conv_a, b_conv_a, gn_a_g, gn_a_b, w_conv_b, beta_a, beta_b,
              w_out, norm_ffn_g, w_up, w_conv_f, w_down, norm_final_g):
    dt = x_prompt.dtype
    meta = jnp.broadcast_to(meta_tokens.astype(dt)[None], (BATCH, N_META, D_MODEL))
    xp = jnp.concatenate([meta, x_prompt], axis=1)
    xs = x_sample
    pa, pb, pf, sa, sb, sf = [], [], [], [], [], []
    for l in range(DEPTH):
        lw = (norm_mix_g[l], w_in[l], w_conv_a[l], b_conv_a[l], gn_a_g[l], gn_a_b[l],
              w_conv_b[l], beta_a[l], beta_b[l], w_out[l], norm_ffn_g[l], w_up[l],
              w_conv_f[l], w_down[l])
        zero_a = jnp.zeros((BATCH, CONV_A_W - 1, A_WIDTH), dt)
        zero_b = jnp.zeros((BATCH, CONV_B_W - 1, B_WIDTH), dt)
        zero_f = jnp.zeros((BATCH, CONV_F_W - 1, 2 * D_FF), dt)
        xp, na, nb, nf = _layer(xp, zero_a, zero_b, zero_f, *lw)
        pa.append(na); pb.append(nb); pf.append(nf)
        xs, na, nb, nf = _layer(xs, state_conv_a[l], state_conv_b[l], state_conv_ffn[l], *lw)
        sa.append(na); sb.append(nb); sf.append(nf)
    y_prompt = _rmsnorm(xp[:, N_META:, :], norm_final_g)
    y_sample = _rmsnorm(xs, norm_final_g)
    return (y_prompt, y_sample, jnp.stack(pa), jnp.stack(pb), jnp.stack(pf),
            jnp.stack(sa), jnp.stack(sb), jnp.stack(sf))
```

```python
import numpy as np
import concourse.bass as bass
import concourse.mybir as mybir
from concourse.bass_utils import run_bass_kernel_spmd

F32 = mybir.dt.float32
BF16 = mybir.dt.bfloat16
F32R = mybir.dt.float32r
AF = mybir.ActivationFunctionType
ALU = mybir.AluOpType

D = 1024
NMETA = 16
SEQ = 2048
NPOS = NMETA + SEQ
NSEQ = 16
LS = 8
AW = 512
IN_COLS = 2560
DFF = 2816
EPS = 1e-6
NCORES = 8

PASSES = [(0, 656, False), (656, 768, False), (1424, 640, True)]
MAXSP = 768
MAXCOLS = 784
NTT = 9
NSLOT = 10
TW = 384

PA = 0
PBC = 124
PGG = 128
PGB = 132
PBA = 136
PBB = 140
PWB = 144
PWF = 156
NPRM = 288


class Op:
    __slots__ = ("eng", "fn", "dma", "deps", "need", "sem", "val")

    def __init__(self, eng, fn, dma):
        self.eng = eng
        self.fn = fn
        self.dma = dma
        self.deps = []
        self.need = False
        self.sem = None
        self.val = None


class Prog:
    ENGS = ("pe", "act", "dve", "pool", "sp")

    def __init__(self):
        self.ops = []
        self.lastw = {}
        self.readers = {}
        self.dma_last = {}

    def add(self, eng, fn, reads=(), writes=(), dma=None):
        op = Op(eng, fn, dma)
        deps = []
        for k in reads:
            w = self.lastw.get(k)
            if w is not None:
                deps.append(w)
        for k in writes:
            w = self.lastw.get(k)
            if w is not None:
                deps.append(w)
            rd = self.readers.get(k)
            if rd:
                for r in rd.values():
                    if isinstance(r, list):
                        deps.extend(r)
                    else:
                        deps.append(r)
        if dma is not None:
            p = self.dma_last.get(dma)
            if p is not None:
                deps.append(p)
            self.dma_last[dma] = op
        seen = set()
        for d in deps:
            if id(d) in seen:
                continue
            seen.add(id(d))
            if d.dma is None and dma is None and d.eng == "pe" and eng == "pe":
                continue
            op.deps.append(d)
            d.need = True
        for k in reads:
            rd = self.readers.setdefault(k, {})
            if dma is not None:
                rd.setdefault("dma", []).append(op)
            else:
                rd[eng] = op
        for k in writes:
            self.lastw[k] = op
            self.readers[k] = {}
        self.ops.append(op)
        return op

    def assign(self):
        cnt = {e: 0 for e in self.ENGS}
        dcnt = {}
        for op in self.ops:
            if op.dma is not None:
                dcnt[op.dma] = dcnt.get(op.dma, 0) + 16
                op.sem = ("dma", op.dma)
                op.val = dcnt[op.dma]
            elif op.need:
                cnt[op.eng] += 1
                op.sem = ("eng", op.eng)
                op.val = cnt[op.eng]
        self.final_dma = dict(dcnt)
        return sorted(dcnt.keys())

    def emit_engine(self, eng, e, semobj, final_waits=False):
        waited = {}
        for op in self.ops:
            if op.eng != eng:
                continue
            for d in op.deps:
                if waited.get(d.sem, 0) < d.val:
                    e.wait_ge(semobj[d.sem], d.val)
                    waited[d.sem] = d.val
            ins = op.fn(e)
            if op.sem is not None:
                ins.then_inc(semobj[op.sem], 16 if op.dma is not None else 1)
        if final_waits:
            for name, v in self.final_dma.items():
                k = ("dma", name)
                if waited.get(k, 0) < v:
                    e.wait_ge(semobj[k], v)


class Arena:
    def __init__(self, ap_f32, nbytes):
        self.ap = ap_f32
        self.nbytes = nbytes
        self.off = 0

    def mark(self):
        return self.off

    def reset(self, off):
        self.off = off

    def alloc(self, nbytes):
        nbytes = (nbytes + 63) // 64 * 64
        o = self.off
        self.off += nbytes
        assert self.off <= self.nbytes, f"arena overflow {self.off} > {self.nbytes}"
        return o

    def f32(self, n):
        o = self.alloc(n * 4)
        return self.ap[:, o // 4:o // 4 + n]

    def bf16(self, n):
        o = self.alloc(n * 2)
        return self.ap[:, o // 4:o // 4 + (n * 2 + 3) // 4].bitcast(BF16)[:, 0:n]


def build(debug=False):
    nc = bass.Bass("TRN2", target_bir_lowering=False)

    def din(name, shape):
        return nc.dram_tensor(name, list(shape), F32, kind="ExternalInput").ap()

    def dout(name, shape):
        return nc.dram_tensor(name, list(shape), F32, kind="ExternalOutput").ap()

    xp = din("xp", [SEQ, D])
    xs = din("xs", [NSEQ * LS, D])
    sa = din("sa", [NSEQ * 30, AW])
    sb_ = din("sb", [NSEQ * 2, AW])
    sf = din("sf", [NSEQ * 2, 2 * DFF])
    meta = din("meta", [NMETA, D])
    g1 = din("g1", [1, D])
    g2 = din("g2", [1, D])
    g3 = din("g3", [1, D])
    w_in = din("w_in", [D, IN_COLS])
    w_out = din("w_out", [D, D])
    w_up = din("w_up", [D, 2 * DFF])
    w_down = din("w_down", [DFF, D])
    wca = din("wca", [124, 128])
    pmisc = din("pmisc", [32, 128])
    wcf = din("wcf", [132, 128])

    yp = dout("yp", [SEQ, D])
    ys = dout("ys", [NSEQ * LS, D])
    ncap = dout("ncap", [30, AW])
    ncbp = dout("ncbp", [2, AW])
    ncfp = dout("ncfp", [2, 2 * DFF])
    ncas = dout("ncas", [NSEQ * 30, AW])
    ncbs = dout("ncbs", [NSEQ * 2, AW])
    ncfs = dout("ncfs", [NSEQ * 2, 2 * DFF])

    ARENA_BYTES = 203000
    LNW = 5 * TW + 256
    lnbuf_t = nc.alloc_sbuf_tensor("lnbuf", [128, LNW], F32R)
    lnbuf = lnbuf_t[:, :]
    t_ua = [lnbuf[:, i * TW:(i + 1) * TW] for i in range(2)]
    t_dsq = [lnbuf[:, (2 + i) * TW:(3 + i) * TW] for i in range(3)]
    Cmat = lnbuf[:, 5 * TW:5 * TW + 128]
    Bd_r = lnbuf[:, 5 * TW + 128:5 * TW + 256]
    arena_t = nc.alloc_sbuf_tensor("arena", [128, ARENA_BYTES // 4], F32)
    A = Arena(arena_t[:, :], ARENA_BYTES)
    ps_t = nc.alloc_psum_tensor("ps", [128, 8, 512], F32)
    ps = ps_t[:, :, :]

    gb = [A.f32(D) for _ in range(3)]
    ident_f = A.f32(128)
    ident_b = A.bf16(128)
    Bd = A.f32(128)
    Cf = A.f32(128)
    prm = A.f32(NPRM)
    ubuf_p = A.bf16(4 * (30 + MAXSP)).rearrange("p (c n) -> p c n", c=4)
    ubuf_s = A.bf16(4 * NSEQ * 38).rearrange("p (c s n) -> p c s n", c=4, s=NSEQ)
    u32 = A.f32(4 * 158).rearrange("p (c n) -> p c n", c=4)
    hist_f = A.f32(44 * 2).rearrange("p (c n) -> p c n", c=44)
    hist_s = A.f32(44 * 32).rearrange("p (c s n) -> p c s n", c=44, s=NSEQ)
    zhist = A.f32(4 * 2).rearrange("p (c n) -> p c n", c=4)
    zs_hist = A.f32(4 * 32).rearrange("p (c s n) -> p c s n", c=4, s=NSEQ)
    zst = A.f32(4 * 34).rearrange("p (c n) -> p c n", c=4)
    stat = A.f32(64)
    cbv = stat[:, 32:36]
    xres = A.f32(NTT * D).rearrange("p (j d) -> p j d", j=NTT)
    hT = A.bf16(8 * MAXCOLS).rearrange("p (k n) -> p k n", k=8)
    ring = [A.bf16(8 * 256).rearrange("p (k n) -> p k n", k=8) for _ in range(NSLOT)]
    hn = [A.bf16(D) for _ in range(2)]
    junk = A.bf16(D)
    scratch = A.f32(16)
    eps_ap = A.f32(1)

    umark = A.mark()
    mixT = A.bf16(8 * MAXCOLS).rearrange("p (k n) -> p k n", k=8)
    zbuf = [A.f32(2 + MAXSP) for _ in range(2)]
    zbuf_s = [A.f32(NSEQ * 10).rearrange("p (s n) -> p s n", s=NSEQ) for _ in range(2)]
    accB = [A.f32(TW) for _ in range(2)]
    t_sig = [A.f32(TW) for _ in range(2)]
    t_cg = [A.f32(TW) for _ in range(2)]
    t_bg = [A.f32(TW) for _ in range(2)]
    t_yn = [A.f32(TW) for _ in range(2)]
    dgraw = A.f32(124 * 128 // 2)
    dg = dgraw.bitcast(BF16).rearrange("p (m n) -> p m n", m=124)
    otokM = [dgraw[:, i * 512:(i + 1) * 512] for i in range(2)]
    m_end = A.mark()
    A.reset(umark)
    actT = A.bf16(22 * MAXCOLS).rearrange("p (k n) -> p k n", k=22)
    SPC = PASSES[-1][1]
    raw = [A.f32(936) for _ in range(4)]
    raw_s = [r[:, 2 + SPC:2 + SPC + NSEQ * 10].rearrange("p (s n) -> p s n", s=NSEQ) for r in raw]
    acc = [A.f32(820) for _ in range(4)]
    otokF = [raw[2 + i][:, 0:512] for i in range(2)]
    stateF = A.f32(44 * 34).rearrange("p (c n) -> p c n", c=44)
    f_end = A.mark()
    A.reset(umark)
    sa_tok = A.f32(4 * AW).rearrange("p (g n) -> p g n", g=4)
    pstage = A.f32(5 * 128).rearrange("p (g n) -> p g n", g=5)
    sb_tok = A.f32(AW)
    sf_tok = [A.f32(512) for _ in range(2)]
    assert A.mark() <= umark + 22 * MAXCOLS * 2
    A.reset(max(m_end, f_end))

    P = Prog()
    LM, LF = "LM", "LF"

    def psb(b):
        return ps[:, b, :]

    def psb_bf(b):
        return ps[:, b, :].bitcast(BF16)

    w_in_v = w_in.rearrange("(k p) c -> p k c", p=128)
    w_out_v = w_out.rearrange("(k p) c -> p k c", p=128)
    w_up_v = w_up.rearrange("(k p) c -> p k c", p=128)
    w_down_v = w_down.rearrange("(k p) c -> p k c", p=128)

    pieces = []
    for _ in range(len(PASSES)):
        for q in (2, 0, 3, 1, 6, 8, 4, 7, 9, 5):
            pieces.append((w_in_v[:, :, q * 256:(q + 1) * 256], 8))
        for q in range(4):
            pieces.append((w_out_v[:, :, q * 256:(q + 1) * 256], 8))
        for g in range(11):
            pieces.append((w_up_v[:, :, g * 256:(g + 1) * 256], 8))
            pieces.append((w_up_v[:, :, DFF + g * 256:DFF + (g + 1) * 256], 8))
        for q in range(4):
            for j in range(3):
                nk = 8 if j < 2 else 6
                pieces.append((w_down_v[:, 8 * j:8 * j + nk, q * 256:(q + 1) * 256], nk))
    wstate = {"issued": 0}

    def w_issue(n):
        src, nk = pieces[n]
        slot = n % NSLOT
        dst = ring[slot][:, 0:nk, :]
        P.add("pool", lambda e, dst=dst, src=src: e.dma_start(out=dst, in_=src),
              writes=[f"w{slot}"], dma=f"ws{slot}")

    def w_prefetch_upto(n):
        while wstate["issued"] <= n and wstate["issued"] < len(pieces):
            w_issue(wstate["issued"])
            wstate["issued"] += 1

    def w_release(n):
        w_prefetch_upto(n + NSLOT)

    wcur = {"n": 0}

    def w_next():
        n = wcur["n"]
        wcur["n"] += 1
        assert n < wstate["issued"]
        return n, ring[n % NSLOT], f"w{n % NSLOT}"

    w_prefetch_upto(1)

    def memset(eng, ap, val, writes, reads=()):
        P.add(eng, lambda e, ap=ap, val=val: e.memset(ap, val), reads=reads, writes=writes)

    memset("dve", ident_f, 0.0, ["ident_f"])
    P.add("pool", lambda e: e.affine_select(out=ident_f, in_=ident_f, compare_op=ALU.not_equal, fill=1.0,
                                            base=0, pattern=[[-1, 128]], channel_multiplier=1),
          writes=["ident_f"])
    P.add("dve", lambda e: e.tensor_copy(out=ident_b, in_=ident_f), reads=["ident_f"], writes=["ident_b"])
    memset("dve", Bd, 0.0, ["Bd"])
    memset("dve", Bd[0:64, 0:64], 1.0 / 64, ["Bd"])
    memset("dve", Bd[64:128, 64:128], 1.0 / 64, ["Bd"])
    P.add("dve", lambda e: e.tensor_tensor(out=Cmat, in0=ident_f, in1=Bd, op=ALU.subtract),
          reads=["ident_f", "Bd"], writes=["Cmat"])
    P.add("dve", lambda e: e.tensor_copy(out=Bd_r, in_=Bd), reads=["Bd"], writes=["Bd_r"])
    P.add("dve", lambda e: e.tensor_tensor(out=Cf, in0=ident_f, in1=Bd, op=ALU.subtract),
          reads=["ident_f", "Bd"], writes=["Cf"])
    memset("dve", eps_ap, float(EPS), ["eps"])
    memset("dve", hist_f, 0.0, ["hist_f"])
    memset("dve", zhist, 0.0, ["zhist"])
    memset("dve", ubuf_p[:, :, 0:30], 0.0, [f"uh{c}" for c in range(4)])

    ldn = {"i": 0}

    def sp_dma(out, in_, reads=(), writes=(), pool="ld", npool=16):
        nm = f"{pool}{ldn['i'] % npool}"
        ldn["i"] += 1
        return P.add("sp", lambda e, out=out, in_=in_: e.dma_start(out=out, in_=in_),
                     reads=reads, writes=writes, dma=nm)

    SLOT_BASE = [0, 6, 3]
    PAR = {}

    def pass_tiles(pi):
        p0, sp, samp = PASSES[pi]
        tt = []
        col = 0
        if pi == 0:
            tt.append((SLOT_BASE[pi] % NTT, 0, NMETA, "m"))
            col = NMETA
        while col < sp:
            tt.append(((SLOT_BASE[pi] + len(tt)) % NTT, col, 128, "p"))
            col += 128
        assert col == sp
        if samp:
            tt.append(((SLOT_BASE[pi] + len(tt)) % NTT, sp, 128, "s"))
        for i, tl in enumerate(tt):
            PAR[(pi, tl[0])] = i % 2
        return tt

    def load_tile(pi, tile):
        p0, sp, samp = PASSES[pi]
        (j, col, nt, kind) = tile
        if kind == "s":
            sp_dma(xres[0:128, j, :], xs, writes=[f"xr{j}"])
        elif kind == "m":
            sp_dma(xres[0:NMETA, j, :], meta, writes=[f"xr{j}"])
        else:
            pos = p0 + col
            sp_dma(xres[0:nt, j, :], xp[pos - NMETA:pos - NMETA + nt, :], writes=[f"xr{j}"])

    sp_dma(pstage[0:124, 0, :], wca, reads=[LF], writes=["pstage0"])
    sp_dma(pstage[0:32, 1, :], pmisc, reads=[LF], writes=["pstage1"])
    for k in range(3):
        sp_dma(pstage[0:44, 2 + k, :], wcf[44 * k:44 * (k + 1), :], reads=[LF], writes=[f"pstage{2 + k}"])
    preloaded = set()
    _t0 = pass_tiles(0)
    load_tile(0, _t0[0])
    preloaded.add((0, _t0[0][0]))
    sp_dma(gb[0], bass.AP(g1.tensor, 0, [[0, 128], [1, D]]), writes=["gb0"])
    for tl in _t0[1:]:
        load_tile(0, tl)
        preloaded.add((0, tl[0]))
    P.add("pool", lambda e: e.memset(scratch[:, 15:16], 0.0), reads=[f"xr{tl[0]}" for tl in _t0[:4]], writes=["wgate"])
    w_prefetch_upto(NSLOT - 1)

    def emit_prm():
        pcols = [(0, 124, 0), (1, 32, 124), (2, 44, 156), (3, 44, 200), (4, 44, 244)]

        def f_prm_T(e):
            ins = None
            for (g, rows, col) in pcols:
                ins = e.transpose(out=ps[:, 0, col:col + rows], in_=pstage[0:rows, g, :], identity=ident_f[0:rows, 0:rows])
            return ins
        P.add("pe", f_prm_T, reads=[f"pstage{g}" for g in range(5)] + ["ident_f", LF], writes=["ps0"])
        P.add("act", lambda e: e.activation(out=prm, in_=ps[:, 0, 0:NPRM], func=AF.Copy), reads=["ps0"], writes=["prm"])
        P.add("pe", lambda e: e.matmul(ps[:, 1, 0:4], lhsT=Cf, rhs=prm[:, PBC:PBC + 4], start=True, stop=True),
              reads=["Cf", "prm"], writes=["ps1"])
        P.add("act", lambda e: e.activation(out=cbv, in_=ps[:, 1, 0:4], func=AF.Copy), reads=["ps1"], writes=["cbv"])


    state_groups = []
    for g in range(4):
        sp_dma(sa_tok[0:120, g, :], sa[120 * g:120 * (g + 1), :], reads=[LF], writes=[f"sa_tok{g}"])
    sp_dma(ncas.rearrange("(s t) c -> s t c", t=30)[:, 0:22, :], sa.rearrange("(s t) c -> s t c", t=30)[:, 8:30, :])
    sp_dma(sb_tok[0:32, :], sb_, reads=[LF], writes=["sb_tok"])

    for i, g in enumerate((g1, g2, g3)):
        if i > 0:
            sp_dma(gb[i], bass.AP(g.tensor, 0, [[0, 128], [1, D]]), writes=[f"gb{i}"])

    def sf_load(g):
        sp_dma(sf_tok[g % 2][0:32, :], sf[:, 512 * g:512 * (g + 1)], reads=[LF], writes=[f"sf_tok{g % 2}"])

    def grp_sa(c):
        b = 4 + c % 2

        def f_saT(e):
            ins = None
            for g in range(4):
                ins = e.transpose(out=ps[:, b, 120 * g:120 * (g + 1)], in_=sa_tok[0:120, g, c * 128:(c + 1) * 128],
                                  identity=ident_f[0:120, 0:120])
            return ins
        P.add("pe", f_saT, reads=[f"sa_tok{g}" for g in range(4)] + ["ident_f", LF], writes=[f"ps{b}"])
        P.add("act", lambda e: e.activation(
            out=ubuf_s[:, c, :, 0:30], in_=ps[:, b, 0:480].rearrange("p (s n) -> p s n", s=NSEQ), func=AF.Copy),
            reads=[f"ps{b}"], writes=[f"ush{c}"])

    def grp_sb():
        def f_sbT(e):
            ins = None
            for c in range(4):
                ins = e.transpose(out=ps[:, 6, 32 * c:32 * (c + 1)], in_=sb_tok[0:32, c * 128:(c + 1) * 128],
                                  identity=ident_f[0:32, 0:32])
            return ins
        P.add("pe", f_sbT, reads=["sb_tok", "ident_f", LF], writes=["ps6"])
        P.add("act", lambda e: e.activation(out=zs_hist.rearrange("p c s n -> p (c s n)"), in_=ps[:, 6, 0:128], func=AF.Copy),
              reads=["ps6"], writes=["zs_hist"])

    def grp_sf(g):
        st = sf_tok[g % 2]
        b = 6 + (g + 1) % 2

        def f_sfT(e):
            ins = None
            for c in range(4):
                ins = e.transpose(out=ps[:, b, 32 * c:32 * (c + 1)], in_=st[0:32, c * 128:(c + 1) * 128],
                                  identity=ident_f[0:32, 0:32])
            return ins
        P.add("pe", f_sfT, reads=[f"sf_tok{g % 2}", "ident_f", LF], writes=[f"ps{b}"])
        P.add("act", lambda e: e.activation(
            out=hist_s[:, 4 * g:4 * g + 4, :, :].rearrange("p c s n -> p (c s n)"), in_=ps[:, b, 0:128], func=AF.Copy),
            reads=[f"ps{b}"], writes=["hist_s"])
        if g + 2 < 11:
            sf_load(g + 2)

    for c in range(4):
        state_groups.append(lambda c=c: grp_sa(c))
    state_groups.append(grp_sb)
    for g in range(11):
        state_groups.append(lambda g=g: grp_sf(g))

    fence_n = {"i": 0}

    def fence():
        i = fence_n["i"] % 16
        fence_n["i"] += 1
        P.add("pool", lambda e, i=i: e.memset(scratch[:, i:i + 1], 0.0), writes=[LM, LF])

    def norm_stats(j, nt, gi, bsel):
        ss = stat[:, 2 * j:2 * j + 1]
        rs = stat[:, 2 * j + 1:2 * j + 2]
        h = hn[bsel]
        P.add("act", lambda e: e.activation(out=junk[0:nt, :], in_=xres[0:nt, j, :], func=AF.Square,
                                            accum_out=ss[0:nt, :]),
              reads=[f"xr{j}"], writes=["junk", f"st{j}"])
        P.add("act", lambda e: e.activation(out=ss[0:nt, :], in_=ss[0:nt, :], func=AF.Sqrt, scale=1.0 / D, bias=eps_ap[0:nt, :]),
              reads=[f"st{j}", "eps"], writes=[f"st{j}"])
        P.add("dve", lambda e: e.reciprocal(out=rs[0:nt, :], in_=ss[0:nt, :]),
              reads=[f"st{j}"], writes=[f"rs{j}"])
        P.add("dve", lambda e: e.scalar_tensor_tensor(out=h[0:nt, :], in0=xres[0:nt, j, :], scalar=rs[0:nt, :],
                                                      in1=gb[gi][0:nt, :], op0=ALU.mult, op1=ALU.mult),
              reads=[f"xr{j}", f"rs{j}", f"gb{gi}"], writes=[f"hn{bsel}"])

    def norm_trans(j, nt, col, bsel, b, evac="act"):
        h = hn[bsel]
        pv = psb_bf(b).rearrange("p (k n) -> p k n", k=8)

        def f_T(e):
            ins = None
            for k in range(8):
                ins = e.transpose(out=pv[:, k, 0:nt], in_=h[0:nt, k * 128:(k + 1) * 128], identity=ident_b[0:nt, 0:nt])
            return ins
        P.add("pe", f_T, reads=[f"hn{bsel}", "ident_b"], writes=[f"ps{b}"])
        if evac == "dve":
            P.add("dve", lambda e: e.tensor_copy(out=hT[:, :, col:col + nt], in_=pv[:, :, 0:nt]),
                  reads=[f"ps{b}"], writes=[f"hT{j}"])
        else:
            P.add("act", lambda e: e.activation(out=hT[:, :, col:col + nt], in_=pv[:, :, 0:nt], func=AF.Copy),
                  reads=[f"ps{b}"], writes=[f"hT{j}"])

    def f_mm(e, r, b, off, n0, nw):
        ins = None
        for k in range(8):
            ins = e.matmul(ps[:, b, 0:nw], lhsT=r[:, k, off:off + 128], rhs=hT[:, k, n0:n0 + nw],
                           start=(k == 0), stop=(k == 7))
        return ins

    p1_done = set()
    pend = []
    LAG_S, LAG_T = 2, 4

    def pend_step(newtile=None, tp=None, allow_trans=True):
        for ent in pend:
            ent[1] += 1
        if allow_trans:
            for ent in list(pend):
                tl, tp_ = ent[0], ent[3]
                if ent[2] and ent[1] >= LAG_T and ent[1] - ent[4] >= 3:
                    norm_trans(tl[0], tl[2], tl[1], PAR[(tp_, tl[0])], 4 + PAR[(tp_, tl[0])], evac="dve")
                    pend.remove(ent)
        for ent in pend:
            tl, tp_ = ent[0], ent[3]
            if not ent[2] and ent[1] >= LAG_S:
                if any(o[2] and PAR[(o[3], o[0][0])] == PAR[(tp_, tl[0])] for o in pend if o is not ent):
                    continue
                norm_stats(tl[0], tl[2], 0, PAR[(tp_, tl[0])])
                ent[2] = True
                ent[4] = ent[1]
        if newtile is not None:
            load_tile(tp, newtile)
            p1_done.add((tp, newtile[0]))
            pend.append([newtile, 0, False, tp, 0])
    for pi, (p0, sp, samp) in enumerate(PASSES):
        last_pass = pi == len(PASSES) - 1
        ncols = sp + (128 if samp else 0)
        ttiles = pass_tiles(pi)
        half = sp // 2
        ntiles = [(0, 0, half, "p"), (1, half, sp - half, "p")]
        if samp:
            ntiles.append((2, sp, 128, "s"))

        def tiles_in(n0, nw):
            return [j for (j, col, nt, kind) in ttiles if col < n0 + nw and col + nt > n0]

        def ntiles_of(col, nt):
            return [t for (t, n0, nw, kind) in ntiles if n0 < col + nt and n0 + nw > col]

        todo = [tl for tl in ttiles if (pi, tl[0]) not in p1_done]
        for tl in todo:
            if (pi, tl[0]) not in preloaded:
                load_tile(pi, tl)
        if pi == 0:
            sf_load(0)
            sf_load(1)
        par = lambda j, pi=pi: PAR[(pi, j)]
        for i in range(len(todo) + 2):
            if i >= 2:
                (j, col, nt, kind) = todo[i - 2]
                norm_trans(j, nt, col, par(j), 4 + par(j), evac="dve")
            if i < len(todo):
                (j, col, nt, kind) = todo[i]
                norm_stats(j, nt, 0, par(j))

        if pi == 0:
            emit_prm()
        fence()

        if pi > 0:
            psp = PASSES[pi - 1][1]
            for c in range(4):
                P.add("pool", lambda e, c=c, psp=psp: e.tensor_copy(out=ubuf_p[:, c, 0:30], in_=ubuf_p[:, c, psp:psp + 30]),
                      reads=[f"u{c}_{t}" for t in range(3)] + [LM], writes=[f"uh{c}"])
        dg_todo = []
        for c in range(4):
            for k0 in range(0, 31, 8):
                k1 = min(31, k0 + 8)
                dg_todo.append((c, k0, k1))

        def dg_piece():
            if not dg_todo:
                return
            c, k0, k1 = dg_todo.pop(0)
            nk = k1 - k0
            wcol = prm[:, PA:PA + 124].rearrange("p (k c) -> p k c", c=4)[:, k0:k1, c:c + 1].broadcast_to([128, nk, 128])
            P.add("dve", lambda e: e.tensor_tensor(
                out=dg[:, c * 31 + k0:c * 31 + k1, :], in0=Cf.unsqueeze(1).broadcast_to([128, nk, 128]), in1=wcol, op=ALU.mult),
                reads=["Cf", "prm", LM], writes=[f"dg{c}"])

        job = 0
        wA = {}
        for cp in range(2):
            wA[("g", cp)] = w_next()
            wA[("v", cp)] = w_next()
        if pi == 0:
            order2a = [(tl_, c) for cp_ in range(2) for tl_ in ntiles for c in (2 * cp_, 2 * cp_ + 1)]
        else:
            order2a = [(tl_, c) for tl_ in ntiles for c in range(4)]
        for ((t, n0, nw, kind), c) in order2a:
            if True:
                cp, cc = c // 2, c % 2
                off = cc * 128
                ng, rg, kg = wA[("g", cp)]
                nv, rv, kv = wA[("v", cp)]
                if True:
                    need_ = set(tiles_in(n0, nw))
                    while any(ent[0][0] in need_ for ent in pend):
                        pend_step()
                    bG = job % 2
                    bV = 2 + job % 2
                    sl = job % 2
                    job += 1
                    hkeys = [f"hT{j}" for j in tiles_in(n0, nw)]
                    P.add("pe", lambda e, r=rg, b=bG, off=off, n0=n0, nw=nw: f_mm(e, r, b, off, n0, nw),
                          reads=[kg] + hkeys, writes=[f"ps{bG}"])
                    P.add("pe", lambda e, r=rv, b=bV, off=off, n0=n0, nw=nw: f_mm(e, r, b, off, n0, nw),
                          reads=[kv] + hkeys, writes=[f"ps{bV}"])
                    P.add("act", lambda e, b=bG, sl=sl, nw=nw: e.activation(out=t_sig[sl][:, 0:nw], in_=ps[:, b, 0:nw],
                                                                             func=AF.Sigmoid),
                          reads=[f"ps{bG}", LM], writes=[f"sig{sl}"])
                    if kind == "p":
                        uo = ubuf_p[:, c, 30 + n0:30 + n0 + nw]
                        i0 = ps[:, bV, 0:nw]
                        i1 = t_sig[sl][:, 0:nw]
                    else:
                        uo = ubuf_s[:, c, :, 30:38]
                        i0 = ps[:, bV, 0:128].rearrange("p (s n) -> p s n", s=NSEQ)
                        i1 = t_sig[sl][:, 0:128].rearrange("p (s n) -> p s n", s=NSEQ)
                    P.add("dve", lambda e, uo=uo, i0=i0, i1=i1: e.tensor_tensor(out=uo, in0=i0, in1=i1, op=ALU.mult),
                          reads=[f"ps{bV}", f"sig{sl}"], writes=[f"u{c}_{t}"])
                    dg_piece()
                    for _ in range(2):
                        if state_groups:
                            state_groups.pop(0)()
                    if job >= 2 and pend:
                        pend_step()
                    if last_pass:
                        if kind == "s":
                            P.add("dve", lambda e, c=c, b=bV, sl=sl: e.tensor_tensor(
                                out=u32[:, c, 30:158], in0=ps[:, b, 0:128], in1=t_sig[sl][:, 0:128], op=ALU.mult),
                                reads=[f"ps{bV}", f"sig{sl}"], writes=[f"u32_{c}"])
                        elif t == 1:
                            P.add("dve", lambda e, c=c, b=bV, sl=sl, nw=nw: e.tensor_tensor(
                                out=u32[:, c, 0:30], in0=ps[:, b, nw - 30:nw], in1=t_sig[sl][:, nw - 30:nw], op=ALU.mult),
                                reads=[f"ps{bV}", f"sig{sl}"], writes=[f"u32_{c}"])
        for cp in range(2):
            w_release(wA[("g", cp)][0])
            w_release(wA[("v", cp)][0])
        while pend:
            pend_step()

        bjobs = []
        wB = {}
        for cp in range(2):
            for cc in range(2):
                c = cp * 2 + cc
                for (t, n0, nw, kind) in ntiles:
                    bjobs.append((cp, cc, c, t, n0, nw, kind))

        def b_head(i, samp=samp):
            cp, cc, c, t, n0, nw, kind = bjobs[i]
            off = cc * 128
            zs = c % 2
            if cc == 0 and t == 0:
                wB[cp] = (w_next(), w_next(), w_next())
            wc, wi, wg = wB[cp]
            if t == 0:
                P.add("pool", lambda e: e.tensor_copy(out=zbuf[zs][:, 0:2], in_=zhist[:, c, :]),
                      reads=["zhist", LM], writes=[f"z{zs}"])
                if samp:
                    P.add("pool", lambda e: e.tensor_copy(out=zbuf_s[zs][:, :, 0:2], in_=zs_hist[:, c, :, :]),
                          reads=["zs_hist", LM], writes=[f"zsb{zs}"])
            bC = 4 + i % 2
            bI = 6 + i % 2
            bGt = i % 2
            sl = i % 2
            hkeys = [f"hT{j}" for j in tiles_in(n0, nw)]
            for (w_, b) in ((wc, bC), (wi, bI), (wg, bGt)):
                P.add("pe", lambda e, r=w_[1], b=b: f_mm(e, r, b, off, n0, nw),
                      reads=[w_[2]] + hkeys, writes=[f"ps{b}"])
            P.add("act", lambda e: e.activation(out=t_cg[sl][:, 0:nw], in_=ps[:, bC, 0:nw], func=AF.Copy),
                  reads=[f"ps{bC}", LM], writes=[f"cg{sl}"])
            P.add("act", lambda e: e.activation(out=t_bg[sl][:, 0:nw], in_=ps[:, bGt, 0:nw], func=AF.Copy),
                  reads=[f"ps{bGt}", LM], writes=[f"bg{sl}"])
            if kind == "p":
                zo = zbuf[zs][:, 2 + n0:2 + n0 + nw]
                zi0 = ps[:, bI, 0:nw]
                zi1 = t_cg[sl][:, 0:nw]
                zkey = f"z{zs}"
            else:
                zo = zbuf_s[zs][:, :, 2:10]
                zi0 = ps[:, bI, 0:128].rearrange("p (s n) -> p s n", s=NSEQ)
                zi1 = t_cg[sl][:, 0:128].rearrange("p (s n) -> p s n", s=NSEQ)
                zkey = f"zsb{zs}"
            P.add("dve", lambda e: e.tensor_tensor(out=zo, in0=zi0, in1=zi1, op=ALU.mult),
                  reads=[f"ps{bI}", f"cg{sl}"], writes=[zkey])
            if cc == 1 and t == len(ntiles) - 1:
                for w_ in (wc, wi, wg):
                    w_release(w_[0])

        def b_tail(i, sp=sp, last_pass=last_pass):
            cp, cc, c, t, n0, nw, kind = bjobs[i]
            zs = c % 2
            sl = i % 2
            w0 = prm[:, PWB + 0 * 4 + c:PWB + 0 * 4 + c + 1]
            w1 = prm[:, PWB + 1 * 4 + c:PWB + 1 * 4 + c + 1]
            w2 = prm[:, PWB + 2 * 4 + c:PWB + 2 * 4 + c + 1]
            bb = prm[:, PBB + c:PBB + c + 1]
            if kind == "p":
                taps = [zbuf[zs][:, n0 + k:n0 + k + nw] for k in range(3)]
                ac = accB[sl][:, 0:nw]
                bgv = t_bg[sl][:, 0:nw]
                mo = mixT[:, 4 + c, n0:n0 + nw]
                zkey = f"z{zs}"
            else:
                taps = [zbuf_s[zs][:, :, k:k + 8] for k in range(3)]
                ac = accB[sl][:, 0:128].rearrange("p (s n) -> p s n", s=NSEQ)
                bgv = t_bg[sl][:, 0:128].rearrange("p (s n) -> p s n", s=NSEQ)
                mo = mixT[:, 4 + c, n0:n0 + 128].rearrange("p (s n) -> p s n", s=NSEQ)
                zkey = f"zsb{zs}"
            P.add("act", lambda e: e.activation(out=ac, in_=taps[0], func=AF.Copy, scale=w0),
                  reads=[zkey, "prm", LM], writes=[f"accB{sl}"])
            for (tp, wk) in ((taps[1], w1), (taps[2], w2)):
                P.add("dve", lambda e, tp=tp, wk=wk: e.scalar_tensor_tensor(out=ac, in0=tp, scalar=wk, in1=ac,
                                                                             op0=ALU.mult, op1=ALU.add),
                      reads=[zkey, "prm", f"accB{sl}"], writes=[f"accB{sl}"])
            P.add("dve", lambda e: e.scalar_tensor_tensor(out=mo, in0=ac, scalar=bb, in1=bgv, op0=ALU.mult, op1=ALU.mult),
                  reads=[f"accB{sl}", f"bg{sl}", "prm"], writes=[f"mx{4 + c}_{t}"])
            if t == len(ntiles) - 1:
                if not last_pass:
                    P.add("pool", lambda e: e.tensor_copy(out=zhist[:, c, :], in_=zbuf[zs][:, sp:sp + 2]),
                          reads=[f"z{zs}"], writes=["zhist"])
                else:
                    P.add("pool", lambda e: e.tensor_copy(out=zst[:, c, 0:2], in_=zbuf[zs][:, sp:sp + 2]),
                          reads=[f"z{zs}"], writes=["zst"])
                    P.add("pool", lambda e: e.tensor_copy(
                        out=zst[:, c, 2:34].rearrange("p (s n) -> p s n", s=NSEQ), in_=zbuf_s[zs][:, :, 8:10]),
                        reads=[f"zsb{zs}"], writes=["zst"])

        if state_groups:
            while state_groups:
                state_groups.pop(0)()
        if pi == 0:
            fence()
        nbj = len(bjobs)
        for i in range(nbj + 1):
            if i >= 1:
                b_tail(i - 1)
            if i < nbj:
                b_head(i)
            dg_piece()
        while dg_todo:
            dg_piece()

        jobs = [(c, t, n0, nw, kind) for (t, n0, nw, kind) in ntiles if kind == "p" for c in range(4)]
        if samp:
            sj = [(c, t, n0, nw, kind) for (t, n0, nw, kind) in ntiles if kind == "s" for c in range(4)]
            merged = []
            for i_, jb_ in enumerate(jobs):
                merged.append(jb_)
                if i_ % 2 == 1 and sj:
                    merged.append(sj.pop(0))
            jobs = merged + sj
        last_job_of = {}
        for i_, jb_ in enumerate(jobs):
            last_job_of[jb_[1]] = i_
        RB = [(t_sig[0], "sig0"), (t_sig[1], "sig1"), (t_cg[0], "cg0")]
        nj = len(jobs)

        def s0(i):
            c, t, n0, nw, kind = jobs[i]
            b = 2 + i % 4
            rk = [f"dg{c}", f"u{c}_{t}", (f"u{c}_{t - 1}" if (kind == "p" and t > 0) else (f"uh{c}" if kind == "p" else f"ush{c}"))]

            def f_conv(e):
                ins = None
                for k in range(31):
                    if kind == "p":
                        rhs = ubuf_p[:, c, n0 + k:n0 + k + nw]
                    else:
                        rhs = ubuf_s[:, c, :, k:k + 8]
                    ins = e.matmul(ps[:, b, 0:nw], lhsT=dg[:, c * 31 + k, :], rhs=rhs, start=(k == 0), stop=(k == 30))
                return ins
            P.add("pe", f_conv, reads=rk + [LM], writes=[f"ps{b}"])

        def s0a(i):
            c, t, n0, nw, kind = jobs[i]
            b = 2 + i % 4
            s3_ = i % 3
            P.add("act", lambda e: e.activation(out=t_dsq[s3_][:, 0:nw], in_=ps[:, b, 0:nw], func=AF.Square,
                                                bias=cbv[:, c:c + 1]),
                  reads=[f"ps{b}", "cbv", LM], writes=[f"dsq{s3_}"])

        def s1(i):
            c, t, n0, nw, kind = jobs[i]
            b = 6 + i % 2
            s3_ = i % 3
            P.add("pe", lambda e: e.matmul(ps[:, b, 0:nw], lhsT=Bd_r, rhs=t_dsq[s3_][:, 0:nw],
                                           start=True, stop=True),
                  reads=["Bd_r", f"dsq{s3_}"], writes=[f"ps{b}"])
            rb, rk_ = RB[s3_]
            P.add("act", lambda e: e.activation(out=rb[:, 0:nw], in_=ps[:, b, 0:nw], func=AF.Sqrt, bias=eps_ap),
                  reads=[f"ps{b}", "eps", LM], writes=[rk_])

        def s2(i):
            c, t, n0, nw, kind = jobs[i]
            bD = 2 + i % 4
            sl = i % 2
            s3_ = i % 3
            rb, rk_ = RB[s3_]
            P.add("dve", lambda e: e.reciprocal(out=rb[:, 0:nw], in_=rb[:, 0:nw]),
                  reads=[rk_], writes=[rk_])
            P.add("dve", lambda e: e.scalar_tensor_tensor(out=t_yn[sl][:, 0:nw], in0=ps[:, bD, 0:nw], scalar=cbv[:, c:c + 1],
                                                          in1=rb[:, 0:nw], op0=ALU.add, op1=ALU.mult),
                  reads=[f"ps{bD}", rk_, "cbv", LM], writes=[f"yn{sl}"])

        def s3(i):
            c, t, n0, nw, kind = jobs[i]
            sl = i % 2
            P.add("act", lambda e: e.activation(out=t_yn[sl][:, 0:nw], in_=t_yn[sl][:, 0:nw], func=AF.Silu,
                                                scale=prm[:, PGG + c:PGG + c + 1], bias=prm[:, PGB + c:PGB + c + 1]),
                  reads=[f"yn{sl}", "prm"], writes=[f"yn{sl}"])
            P.add("dve", lambda e: e.tensor_scalar(out=mixT[:, c, n0:n0 + nw], in0=t_yn[sl][:, 0:nw],
                                                   scalar1=prm[:, PBA + c:PBA + c + 1], scalar2=None, op0=ALU.mult),
                  reads=[f"yn{sl}", "prm", LM], writes=[f"mx{c}_{t}"])

        p3 = {"idx": 0, "jo": 0, "wn4": None}
        KORD = (4, 5, 6, 7, 0, 1, 2, 3)

        def p3_tile():
            if p3["wn4"] is None:
                p3["wn4"] = [w_next() for _ in range(4)]
            wn4 = p3["wn4"]
            idx = p3["idx"]
            p3["idx"] += 1
            (j, col, nt, kind) = ttiles[idx]
            mk = [f"mx{c}_{t}" for c in range(8) for t in ntiles_of(col, nt)]
            for q in range(4):
                b = p3["jo"] % 4
                p3["jo"] += 1

                def f_mo(e, r=wn4[q][1], b=b):
                    ins = None
                    for i_, k in enumerate(KORD):
                        ins = e.matmul(ps[0:nt, b, 0:256], lhsT=mixT[:, k, col:col + nt], rhs=r[:, k, :],
                                       start=(i_ == 0), stop=(i_ == 7))
                    return ins
                P.add("pe", f_mo, reads=[wn4[q][2]] + mk + [LM], writes=[f"ps{b}"])
                P.add("dve", lambda e, b=b, q=q: e.tensor_tensor(
                    out=xres[0:nt, j, 256 * q:256 * (q + 1)], in0=ps[0:nt, b, 0:256], in1=xres[0:nt, j, 256 * q:256 * (q + 1)],
                    op=ALU.add),
                    reads=[f"ps{b}", f"xr{j}"], writes=[f"xr{j}"])
            if idx >= 3:
                (j2, col2, nt2, kind2) = ttiles[idx - 3]
                norm_trans(j2, nt2, col2, par(j2), 4 + par(j2))
            if idx >= 1:
                (j2, col2, nt2, kind2) = ttiles[idx - 1]
                norm_stats(j2, nt2, 1, par(j2))

        for it in range(nj + 4):
            for (fn, lag) in ((s0, 0), (s2, 2), (s3, 3), (s1, 1), (s0a, 0)):
                if 0 <= it - lag < nj:
                    fn(it - lag)
            if it >= nj and p3["idx"] < len(ttiles):
                (j_, col_, nt_, kind_) = ttiles[p3["idx"]]
                if all(it >= last_job_of[t_] + 3 for t_ in ntiles_of(col_, nt_)):
                    p3_tile()

        if last_pass:
            def f_uT(e, lo, n, b):
                ins = None
                for c in range(4):
                    ins = e.transpose(out=ps[0:n, b, c * 128:(c + 1) * 128], in_=u32[:, c, lo:lo + n], identity=ident_f)
                return ins
            P.add("pe", lambda e: f_uT(e, 0, 30, 0), reads=[f"u32_{c}" for c in range(4)] + ["ident_f"], writes=["ps0"])
            P.add("act", lambda e: e.activation(out=otokM[0][0:30, :], in_=ps[0:30, 0, :], func=AF.Copy),
                  reads=["ps0"], writes=["dg0", "otokM0"])
            sp_dma(ncap, otokM[0][0:30, :], reads=["otokM0", LM], pool="st")
            P.add("pe", lambda e: f_uT(e, 30, 128, 1), reads=[f"u32_{c}" for c in range(4)] + ["ident_f"], writes=["ps1"])
            P.add("act", lambda e: e.activation(out=otokM[1][:, :], in_=ps[:, 1, :], func=AF.Copy),
                  reads=["ps1"], writes=["dg0", "otokM1"])
            for s in range(NSEQ):
                sp_dma(ncas[30 * s + 22:30 * s + 30, :], otokM[1][8 * s:8 * s + 8, :], reads=["otokM1", LM], pool="st")

            def f_zT(e):
                ins = None
                for c in range(4):
                    ins = e.transpose(out=ps[0:34, 0, c * 128:(c + 1) * 128], in_=zst[:, c, :], identity=ident_f)
                return ins
            P.add("pe", f_zT, reads=["zst", "ident_f"], writes=["ps0"])
            P.add("act", lambda e: e.activation(out=otokM[0][0:34, :], in_=ps[0:34, 0, :], func=AF.Copy),
                  reads=["ps0"], writes=["dg0", "otokM0"])
            sp_dma(ncbp, otokM[0][0:2, :], reads=["otokM0", LM], pool="st")
            sp_dma(ncbs, otokM[0][2:34, :], reads=["otokM0", LM], pool="st")

        while p3["idx"] < len(ttiles):
            p3_tile()
        for w_ in p3["wn4"]:
            w_release(w_[0])
        ntt_ = len(ttiles)
        for idx in range(ntt_, ntt_ + 3):
            if 0 <= idx - 3 < ntt_:
                (j2, col2, nt2, kind2) = ttiles[idx - 3]
                norm_trans(j2, nt2, col2, par(j2), 4 + par(j2))
            if 0 <= idx - 1 < ntt_:
                (j2, col2, nt2, kind2) = ttiles[idx - 1]
                norm_stats(j2, nt2, 1, par(j2))

        fence()

        wP = {}
        bank4 = {"n": 0}

        def p4_head(j, samp=samp):
            if j % 2 == 0:
                wP[j // 2] = (w_next(), w_next())
            wg_, wv_ = wP[j // 2]
            off = (j % 2) * 128
            for hi, (w_, ci) in enumerate(((wg_, j), (wv_, 22 + j))):
                rs_ = (2 * j + hi) % 4
                P.add("pool", lambda e, rs_=rs_, ci=ci: e.tensor_copy(out=raw[rs_][:, 0:2], in_=hist_f[:, ci, :]),
                      reads=["hist_f", LF], writes=[f"rawh{rs_}"])
                if samp:
                    P.add("dve", lambda e, rs_=rs_, ci=ci: e.tensor_copy(out=raw_s[rs_][:, :, 0:2], in_=hist_s[:, ci, :, :]),
                          reads=["hist_s", LF], writes=[f"rawsh{rs_}"])
                for (t, n0, nw, kind) in ntiles:
                    b = bank4["n"] % 6
                    bank4["n"] += 1
                    hkeys = [f"hT{jj}" for jj in tiles_in(n0, nw)]
                    P.add("pe", lambda e, r=w_[1], b=b, n0=n0, nw=nw: f_mm(e, r, b, off, n0, nw),
                          reads=[w_[2]] + hkeys, writes=[f"ps{b}"])
                    if kind == "p":
                        P.add("act", lambda e, rs_=rs_, b=b, n0=n0, nw=nw: e.activation(
                            out=raw[rs_][:, 2 + n0:2 + n0 + nw], in_=ps[:, b, 0:nw], func=AF.Copy),
                            reads=[f"ps{b}", LF], writes=[f"raw{rs_}"])
                    else:
                        P.add("act", lambda e, rs_=rs_, b=b: e.activation(
                            out=raw_s[rs_][:, :, 2:10], in_=ps[:, b, 0:128].rearrange("p (s n) -> p s n", s=NSEQ), func=AF.Copy),
                            reads=[f"ps{b}", LF], writes=[f"raws{rs_}"])
            if j % 2 == 1:
                w_release(wg_[0])
                w_release(wv_[0])

        def p4_mid(j, sp=sp, last_pass=last_pass, samp=samp):
            for hi, ci in enumerate((j, 22 + j)):
                rs_ = (2 * j + hi) % 4
                wk = [prm[:, PWF + k * 44 + ci:PWF + k * 44 + ci + 1] for k in range(3)]
                fw = sp + (2 + NSEQ * 10 if samp else 0)
                rkeys = [f"raw{rs_}", f"rawh{rs_}"] + ([f"raws{rs_}", f"rawsh{rs_}"] if samp else [])
                P.add("act", lambda e, rs_=rs_, wk=wk, fw=fw: e.activation(out=acc[rs_][:, 0:fw], in_=raw[rs_][:, 0:fw],
                                                                            func=AF.Copy, scale=wk[0]),
                      reads=rkeys + ["prm", LF], writes=[f"acc{rs_}"])
                for k in (1, 2):
                    P.add("dve", lambda e, rs_=rs_, wk=wk, k=k, fw=fw: e.scalar_tensor_tensor(
                        out=acc[rs_][:, 0:fw], in0=raw[rs_][:, k:k + fw], scalar=wk[k], in1=acc[rs_][:, 0:fw],
                        op0=ALU.mult, op1=ALU.add),
                        reads=rkeys + ["prm", f"acc{rs_}"], writes=[f"acc{rs_}"])
                if not last_pass:
                    P.add("pool", lambda e, rs_=rs_, ci=ci: e.tensor_copy(out=hist_f[:, ci, :], in_=raw[rs_][:, sp:sp + 2]),
                          reads=[f"raw{rs_}"], writes=["hist_f"])
                else:
                    P.add("pool", lambda e, rs_=rs_, ci=ci: e.tensor_copy(out=stateF[:, ci, 0:2], in_=raw[rs_][:, sp:sp + 2]),
                          reads=[f"raw{rs_}", LF], writes=["stateF"])
                    P.add("act", lambda e, rs_=rs_, ci=ci: e.activation(
                        out=stateF[:, ci, 2:34].rearrange("p (s n) -> p s n", s=NSEQ), in_=raw_s[rs_][:, :, 8:10], func=AF.Copy),
                        reads=[f"raws{rs_}", LF], writes=["stateF"])

        def p4_tail(j, ncols=ncols, samp=samp, sp=sp):
            rg_ = (2 * j) % 4
            rv_ = (2 * j + 1) % 4
            akeys_g = [f"acc{rg_}"]
            akeys_v = [f"acc{rv_}"]
            fw = sp + (2 + NSEQ * 10 if samp else 0)
            P.add("act", lambda e: e.activation(out=acc[rg_][:, 0:fw], in_=acc[rg_][:, 0:fw], func=AF.Silu),
                  reads=akeys_g, writes=akeys_g)
            P.add("dve", lambda e: e.tensor_tensor(out=actT[:, j, 0:sp], in0=acc[rg_][:, 0:sp], in1=acc[rv_][:, 0:sp], op=ALU.mult),
                  reads=akeys_g + akeys_v + [LF], writes=[f"at{j}"])
            if samp:
                sv = lambda a: a[:, sp + 2:sp + 2 + NSEQ * 10].rearrange("p (s n) -> p s n", s=NSEQ)[:, :, 0:8]
                P.add("dve", lambda e: e.tensor_tensor(out=actT[:, j, sp:sp + 128].rearrange("p (s n) -> p s n", s=NSEQ),
                                                       in0=sv(acc[rg_]), in1=sv(acc[rv_]), op=ALU.mult),
                      reads=akeys_g + akeys_v + [LF], writes=[f"at{j}"])

        if samp:
            for rs_ in range(4):
                P.add("pool", lambda e, rs_=rs_: e.memset(raw[rs_][:, 2 + sp + NSEQ * 10:2 + sp + NSEQ * 10 + 4], 0.0),
                      reads=[LF], writes=[f"raws{rs_}"])
        for it in range(22 + 2):
            if 0 <= it - 2 < 22:
                p4_tail(it - 2)
            if 0 <= it - 1 < 22:
                p4_mid(it - 1)
            if it < 22:
                p4_head(it)

        nxt = pass_tiles(pi + 1) if not last_pass else []
        nxt_by_slot = {tl[0]: tl for tl in nxt}
        def next_pass_step(newslot):
            pend_step(nxt_by_slot.get(newslot) if newslot is not None else None, pi + 1)

        cur_slots = set(tl[0] for tl in ttiles)
        tt5 = [tl for tl in ttiles if tl[3] != "m"]
        early = [tl for tl in nxt if tl[0] not in cur_slots]
        for tl in early:
            load_tile(pi + 1, tl)
            p1_done.add((pi + 1, tl[0]))
            pend.append([tl, 0, False, pi + 1, 0])
        jo = 0
        BK5 = [0, 1, 2, 3, 6, 7, 5]

        def md_add(q, wd3, tile, b, k0=0, k1=22, do_add=True):
            (j, col, nt, kind) = tile

            def f_md(e):
                ins = None
                for k in range(k0, k1):
                    ins = e.matmul(ps[0:nt, b, 0:256], lhsT=actT[:, k, col:col + nt], rhs=wd3[k // 8][1][:, k % 8, :],
                                   start=(k == 0), stop=(k == 21))
                return ins
            P.add("pe", f_md, reads=[w_[2] for w_ in wd3] + [f"at{k}" for k in range(k0, k1)] + [LF], writes=[f"ps{b}"])
            if do_add:
                P.add("dve", lambda e: e.tensor_tensor(
                    out=xres[0:nt, j, 256 * q:256 * (q + 1)], in0=ps[0:nt, b, 0:256], in1=xres[0:nt, j, 256 * q:256 * (q + 1)],
                    op=ALU.add),
                    reads=[f"ps{b}", f"xr{j}"], writes=[f"xr{j}"])

        def final_norm(tile, p0=p0):
            (j, col, nt, kind) = tile
            ss = stat[:, 2 * j:2 * j + 1]
            rs = stat[:, 2 * j + 1:2 * j + 2]
            P.add("act", lambda e: e.activation(out=junk[0:nt, :], in_=xres[0:nt, j, :], func=AF.Square, accum_out=ss[0:nt, :]),
                  reads=[f"xr{j}"], writes=["junk", f"st{j}"])
            P.add("act", lambda e: e.activation(out=ss[0:nt, :], in_=ss[0:nt, :], func=AF.Sqrt, scale=1.0 / D, bias=eps_ap[0:nt, :]),
                  reads=[f"st{j}", "eps"], writes=[f"st{j}"])
            P.add("dve", lambda e: e.reciprocal(out=rs[0:nt, :], in_=ss[0:nt, :]),
                  reads=[f"st{j}"], writes=[f"rs{j}"])
            P.add("dve", lambda e: e.scalar_tensor_tensor(
                out=xres[0:nt, j, :], in0=xres[0:nt, j, :], scalar=rs[0:nt, :], in1=gb[2][0:nt, :], op0=ALU.mult, op1=ALU.mult),
                reads=[f"xr{j}", f"rs{j}", "gb2"], writes=[f"xr{j}"])
            if kind == "s":
                sp_dma(ys, xres[0:128, j, :], reads=[f"xr{j}"], pool="st")
            else:
                pos = p0 + col
                sp_dma(yp[pos - NMETA:pos - NMETA + nt, :], xres[0:nt, j, :], reads=[f"xr{j}"], pool="st")
            next_pass_step(j)

        for tl in ttiles:
            if tl[3] == "m":
                next_pass_step(tl[0])
        wd3 = [w_next() for _ in range(3)]
        for (k0, k1) in ((0, 16), (16, 22)):
            for idx, tile in enumerate(tt5):
                md_add(0, wd3, tile, BK5[idx], k0, k1, do_add=(k1 == 22))
                if k1 == 16 and pend:
                    next_pass_step(None)
        for w_ in wd3:
            w_release(w_[0])
        sf_groups = list(range(11)) if last_pass else []

        def sf_out_group():
            if not sf_groups:
                return
            g = sf_groups.pop(0)
            b = 4 + g % 2

            def f_fT(e):
                ins = None
                for c4 in range(4):
                    ins = e.transpose(out=ps[0:34, b, c4 * 128:(c4 + 1) * 128], in_=stateF[:, 4 * g + c4, :], identity=ident_f)
                return ins
            P.add("pe", f_fT, reads=["stateF", "ident_f", LF], writes=[f"ps{b}"])
            P.add("act", lambda e: e.activation(out=otokF[g % 2][0:34, :], in_=ps[0:34, b, :], func=AF.Copy),
                  reads=[f"ps{b}", LF], writes=[f"raw{2 + g % 2}", f"rawh{2 + g % 2}", f"otokF{g % 2}"])
            sp_dma(ncfp[:, 512 * g:512 * (g + 1)], otokF[g % 2][0:2, :], reads=[f"otokF{g % 2}", LF], pool="st")
            sp_dma(ncfs[:, 512 * g:512 * (g + 1)], otokF[g % 2][2:34, :], reads=[f"otokF{g % 2}", LF], pool="st")

        wd3 = [w_next() for _ in range(3)]
        for tile in tt5:
            md_add(1, wd3, tile, jo % 4)
            jo += 1
            if pend:
                next_pass_step(None)
            sf_out_group()
            sf_out_group()
        for w_ in wd3:
            w_release(w_[0])
        wd3a = [w_next() for _ in range(3)]
        wd3b = [w_next() for _ in range(3)]
        for idx, tile in enumerate(tt5):
            md_add(2, wd3a, tile, jo % 4)
            jo += 1
            md_add(3, wd3b, tile, jo % 4)
            jo += 1
            if idx >= 1:
                final_norm(tt5[idx - 1])
        final_norm(tt5[-1])
        for w_ in wd3a + wd3b:
            w_release(w_[0])

    assert wcur["n"] == len(pieces), (wcur["n"], len(pieces))

    dma_names = P.assign()
    from contextlib import ExitStack
    with ExitStack() as ctx:
        semobj = {}
        for eng in Prog.ENGS:
            semobj[("eng", eng)] = ctx.enter_context(nc.semaphore(f"e_{eng}"))
        for nm in dma_names:
            semobj[("dma", nm)] = ctx.enter_context(nc.semaphore(f"d_{nm}"))
        block = ctx.enter_context(nc.Block())

        @block.tensor
        def _(e):
            P.emit_engine("pe", e, semobj)

        @block.scalar
        def _(e):
            P.emit_engine("act", e, semobj)

        @block.vector
        def _(e):
            P.emit_engine("dve", e, semobj)

        @block.gpsimd
        def _(e):
            P.emit_engine("pool", e, semobj)

        @block.sync
        def _(e):
            P.emit_engine("sp", e, semobj, final_waits=True)
    return nc


_NC_CACHE = {}


def _get_nc():
    if "nc" not in _NC_CACHE:
        _NC_CACHE["nc"] = build()
    return _NC_CACHE["nc"]


def make_in_maps(x_prompt, x_sample, state_conv_a, state_conv_b, state_conv_ffn, meta_tokens,
                 norm_mix_g, w_in, w_conv_a, b_conv_a, gn_a_g, gn_a_b, w_conv_b, beta_a, beta_b,
                 w_out, norm_ffn_g, w_up, w_conv_f, w_down, norm_final_g):
    f = lambda a: np.ascontiguousarray(np.asarray(a, dtype=np.float32))
    pmisc = np.concatenate([f(b_conv_a[0]).reshape(4, 128), f(gn_a_g[0]).reshape(4, 128), f(gn_a_b[0]).reshape(4, 128),
                            f(beta_a[0]).reshape(4, 128), f(beta_b[0]).reshape(4, 128),
                            f(w_conv_b[0]).reshape(12, 128)], axis=0)
    shared = {
        "meta": f(meta_tokens), "g1": f(norm_mix_g).reshape(1, D), "g2": f(norm_ffn_g).reshape(1, D),
        "g3": f(norm_final_g).reshape(1, D), "w_in": f(w_in[0]), "w_out": f(w_out[0]), "w_up": f(w_up[0]),
        "w_down": f(w_down[0]), "wca": f(w_conv_a[0]).reshape(124, 128), "pmisc": f(pmisc),
        "wcf": f(w_conv_f[0]).reshape(132, 128),
    }
    maps = []
    for c in range(NCORES):
        m = dict(shared)
        m["xp"] = f(x_prompt[c])
        m["xs"] = f(x_sample[c * NSEQ:(c + 1) * NSEQ]).reshape(NSEQ * LS, D)
        m["sa"] = f(state_conv_a[0, c * NSEQ:(c + 1) * NSEQ]).reshape(NSEQ * 30, AW)
        m["sb"] = f(state_conv_b[0, c * NSEQ:(c + 1) * NSEQ]).reshape(NSEQ * 2, AW)
        m["sf"] = f(state_conv_ffn[0, c * NSEQ:(c + 1) * NSEQ]).reshape(NSEQ * 2, 2 * DFF)
        maps.append(m)
    return maps


def gather(results):
    cat = lambda k: np.stack([r[k] for r in results], axis=0)
    y_prompt = cat("yp")
    y_sample = cat("ys").reshape(NCORES * NSEQ, LS, D)
    ncap = cat("ncap")[None]
    ncbp = cat("ncbp")[None]
    ncfp = cat("ncfp")[None]
    ncas = cat("ncas").reshape(1, NCORES * NSEQ, 30, AW)
    ncbs = cat("ncbs").reshape(1, NCORES * NSEQ, 2, AW)
    ncfs = cat("ncfs").reshape(1, NCORES * NSEQ, 2, 2 * DFF)
    return tuple(np.ascontiguousarray(a, dtype=np.float32) for a in
                 (y_prompt, y_sample, ncap, ncbp, ncfp, ncas, ncbs, ncfs))


def kernel(**inputs):
    nc = _get_nc()
    in_maps = make_in_maps(**inputs)
    res = run_bass_kernel_spmd(nc, in_maps, core_ids=list(range(NCORES)))
    return gather(res.results)
```

```python
import numpy as np
import concourse.bass as bass
import concourse.mybir as mybir
from concourse.bass_utils import run_bass_kernel_spmd

F32 = mybir.dt.float32
BF16 = mybir.dt.bfloat16
F32R = mybir.dt.float32r
AF = mybir.ActivationFunctionType
ALU = mybir.AluOpType

D = 1024
NMETA = 16
SEQ = 2048
NPOS = NMETA + SEQ
NSEQ = 16
LS = 8
AW = 512
IN_COLS = 2560
DFF = 2816
EPS = 1e-6
NCORES = 8

PASSES = [(0, 656, False), (656, 768, False), (1424, 640, True)]
MAXSP = 768
MAXCOLS = 784
NTT = 9
NSLOT = 11
TW = 384

PA = 0
PBC = 124
PGG = 128
PGB = 132
PBA = 136
PBB = 140
PWB = 144
PWF = 156
NPRM = 288


class Op:
    __slots__ = ("eng", "fn", "dma", "deps", "need", "sem", "val")

    def __init__(self, eng, fn, dma):
        self.eng = eng
        self.fn = fn
        self.dma = dma
        self.deps = []
        self.need = False
        self.sem = None
        self.val = None


class Prog:
    ENGS = ("pe", "act", "dve", "pool", "sp")

    def __init__(self):
        self.ops = []
        self.lastw = {}
        self.readers = {}
        self.dma_last = {}

    def add(self, eng, fn, reads=(), writes=(), dma=None):
        op = Op(eng, fn, dma)
        deps = []
        for k in reads:
            w = self.lastw.get(k)
            if w is not None:
                deps.append(w)
        for k in writes:
            w = self.lastw.get(k)
            if w is not None:
                deps.append(w)
            rd = self.readers.get(k)
            if rd:
                for r in rd.values():
                    if isinstance(r, list):
                        deps.extend(r)
                    else:
                        deps.append(r)
        if dma is not None:
            p = self.dma_last.get(dma)
            if p is not None:
                deps.append(p)
            self.dma_last[dma] = op
        seen = set()
        for d in deps:
            if id(d) in seen:
                continue
            seen.add(id(d))
            if d.dma is None and dma is None and d.eng == "pe" and eng == "pe":
                continue
            op.deps.append(d)
            d.need = True
        for k in reads:
            rd = self.readers.setdefault(k, {})
            if dma is not None:
                rd.setdefault("dma", []).append(op)
            else:
                rd[eng] = op
        for k in writes:
            self.lastw[k] = op
            self.readers[k] = {}
        self.ops.append(op)
        return op

    def assign(self):
        cnt = {e: 0 for e in self.ENGS}
        dcnt = {}
        for op in self.ops:
            if op.dma is not None:
                dcnt[op.dma] = dcnt.get(op.dma, 0) + 16
                op.sem = ("dma", op.dma)
                op.val = dcnt[op.dma]
            elif op.need:
                cnt[op.eng] += 1
                op.sem = ("eng", op.eng)
                op.val = cnt[op.eng]
        self.final_dma = dict(dcnt)
        return sorted(dcnt.keys())

    def emit_engine(self, eng, e, semobj, final_waits=False):
        waited = {}
        for op in self.ops:
            if op.eng != eng:
                continue
            for d in op.deps:
                if waited.get(d.sem, 0) < d.val:
                    e.wait_ge(semobj[d.sem], d.val)
                    waited[d.sem] = d.val
            ins = op.fn(e)
            if op.sem is not None:
                ins.then_inc(semobj[op.sem], 16 if op.dma is not None else 1)
        if final_waits:
            for name, v in self.final_dma.items():
                k = ("dma", name)
                if waited.get(k, 0) < v:
                    e.wait_ge(semobj[k], v)


class Arena:
    def __init__(self, ap_f32, nbytes):
        self.ap = ap_f32
        self.nbytes = nbytes
        self.off = 0

    def mark(self):
        return self.off

    def reset(self, off):
        self.off = off

    def alloc(self, nbytes):
        nbytes = (nbytes + 63) // 64 * 64
        o = self.off
        self.off += nbytes
        assert self.off <= self.nbytes, f"arena overflow {self.off} > {self.nbytes}"
        return o

    def f32(self, n):
        o = self.alloc(n * 4)
        return self.ap[:, o // 4:o // 4 + n]

    def bf16(self, n):
        o = self.alloc(n * 2)
        return self.ap[:, o // 4:o // 4 + (n * 2 + 3) // 4].bitcast(BF16)[:, 0:n]


def build(debug=False):
    nc = bass.Bass("TRN2", target_bir_lowering=False)

    def din(name, shape):
        return nc.dram_tensor(name, list(shape), F32, kind="ExternalInput").ap()

    def dout(name, shape):
        return nc.dram_tensor(name, list(shape), F32, kind="ExternalOutput").ap()

    xp = din("xp", [SEQ, D])
    xs = din("xs", [NSEQ * LS, D])
    sa = din("sa", [NSEQ * 30, AW])
    sb_ = din("sb", [NSEQ * 2, AW])
    sf = din("sf", [NSEQ * 2, 2 * DFF])
    meta = din("meta", [NMETA, D])
    g1 = din("g1", [1, D])
    g2 = din("g2", [1, D])
    g3 = din("g3", [1, D])
    w_in = din("w_in", [D, IN_COLS])
    w_out = din("w_out", [D, D])
    w_up = din("w_up", [D, 2 * DFF])
    w_down = din("w_down", [DFF, D])
    wca = din("wca", [124, 128])
    pmisc = din("pmisc", [32, 128])
    wcf = din("wcf", [132, 128])

    yp = dout("yp", [SEQ, D])
    ys = dout("ys", [NSEQ * LS, D])
    ncap = dout("ncap", [30, AW])
    ncbp = dout("ncbp", [2, AW])
    ncfp = dout("ncfp", [2, 2 * DFF])
    ncas = dout("ncas", [NSEQ * 30, AW])
    ncbs = dout("ncbs", [NSEQ * 2, AW])
    ncfs = dout("ncfs", [NSEQ * 2, 2 * DFF])

    ARENA_BYTES = 206500
    LNW = 3 * TW + 128
    lnbuf_t = nc.alloc_sbuf_tensor("lnbuf", [128, LNW], F32R)
    lnbuf = lnbuf_t[:, :]
    t_dsq = [lnbuf[:, i * TW:(i + 1) * TW] for i in range(3)]
    Bd_r = lnbuf[:, 3 * TW:3 * TW + 128]
    arena_t = nc.alloc_sbuf_tensor("arena", [128, ARENA_BYTES // 4], F32)
    A = Arena(arena_t[:, :], ARENA_BYTES)
    ps_t = nc.alloc_psum_tensor("ps", [128, 8, 512], F32)
    ps = ps_t[:, :, :]

    gb = [A.f32(D) for _ in range(3)]
    ident_f = A.f32(128)
    ident_b = A.bf16(128)
    Bd = A.f32(128)
    Cf = A.f32(128)
    prm = A.f32(NPRM)
    ubuf_p = A.bf16(4 * (30 + MAXSP)).rearrange("p (c n) -> p c n", c=4)
    ubuf_s = A.bf16(4 * NSEQ * 38).rearrange("p (c s n) -> p c s n", c=4, s=NSEQ)
    u32 = A.f32(4 * 158).rearrange("p (c n) -> p c n", c=4)
    hist_f = A.f32(44 * 2).rearrange("p (c n) -> p c n", c=44)
    hist_s = A.f32(44 * 32).rearrange("p (c s n) -> p c s n", c=44, s=NSEQ)
    zhist = A.f32(4 * 2).rearrange("p (c n) -> p c n", c=4)
    zs_hist = A.f32(4 * 32).rearrange("p (c s n) -> p c s n", c=4, s=NSEQ)
    zst = A.f32(4 * 34).rearrange("p (c n) -> p c n", c=4)
    stat = A.f32(64)
    cbv = stat[:, 32:36]
    xres = A.f32(NTT * D).rearrange("p (j d) -> p j d", j=NTT)
    hT = A.bf16(8 * MAXCOLS).rearrange("p (k n) -> p k n", k=8)
    ring = [A.bf16(8 * 256).rearrange("p (k n) -> p k n", k=8) for _ in range(NSLOT)]
    hn = [A.bf16(D) for _ in range(2)]
    junk = A.bf16(D)
    scratch = A.f32(16)
    eps_ap = A.f32(1)

    umark = A.mark()
    mixT = A.bf16(8 * MAXCOLS).rearrange("p (k n) -> p k n", k=8)
    zbuf = [A.f32(2 + MAXSP) for _ in range(2)]
    zbuf_s = [A.f32(NSEQ * 10).rearrange("p (s n) -> p s n", s=NSEQ) for _ in range(2)]
    accB = [A.f32(TW) for _ in range(2)]
    t_sig = [A.f32(TW) for _ in range(2)]
    t_cg = [A.f32(TW) for _ in range(2)]
    t_bg = [A.f32(TW) for _ in range(2)]
    t_yn = [A.f32(TW) for _ in range(2)]
    dgraw = A.f32(124 * 128 // 2)
    dg = dgraw.bitcast(BF16).rearrange("p (m n) -> p m n", m=124)
    otokM = [dgraw[:, i * 512:(i + 1) * 512] for i in range(2)]
    m_end = A.mark()
    A.reset(umark)
    actT = A.bf16(22 * MAXCOLS).rearrange("p (k n) -> p k n", k=22)
    SPC = PASSES[-1][1]
    raw = [A.f32(936) for _ in range(4)]
    raw_s = [r[:, 2 + SPC:2 + SPC + NSEQ * 10].rearrange("p (s n) -> p s n", s=NSEQ) for r in raw]
    acc = [A.f32(820) for _ in range(4)]
    otokF = [raw[2 + i][:, 0:512] for i in range(2)]
    stateF = A.f32(44 * 34).rearrange("p (c n) -> p c n", c=44)
    f_end = A.mark()
    A.reset(umark)
    sa_tok = A.f32(4 * AW).rearrange("p (g n) -> p g n", g=4)
    pstage = A.f32(5 * 128).rearrange("p (g n) -> p g n", g=5)
    sb_tok = A.f32(AW)
    sf_tok = [A.f32(512) for _ in range(2)]
    assert A.mark() <= umark + 22 * MAXCOLS * 2
    A.reset(max(m_end, f_end))

    P = Prog()
    LM, LF = "LM", "LF"

    def psb(b):
        return ps[:, b, :]

    def psb_bf(b):
        return ps[:, b, :].bitcast(BF16)

    w_in_v = w_in.rearrange("(k p) c -> p k c", p=128)
    w_out_v = w_out.rearrange("(k p) c -> p k c", p=128)
    w_up_v = w_up.rearrange("(k p) c -> p k c", p=128)
    w_down_v = w_down.rearrange("(k p) c -> p k c", p=128)

    pieces = []
    for _ in range(len(PASSES)):
        for q in (2, 0, 3, 1, 6, 8, 4, 7, 9, 5):
            pieces.append((w_in_v[:, :, q * 256:(q + 1) * 256], 8))
        for q in range(4):
            pieces.append((w_out_v[:, :, q * 256:(q + 1) * 256], 8))
        for g in range(11):
            pieces.append((w_up_v[:, :, g * 256:(g + 1) * 256], 8))
            pieces.append((w_up_v[:, :, DFF + g * 256:DFF + (g + 1) * 256], 8))
        for q in range(4):
            for j in range(3):
                nk = 8 if j < 2 else 6
                pieces.append((w_down_v[:, 8 * j:8 * j + nk, q * 256:(q + 1) * 256], nk))
    wstate = {"issued": 0}

    def w_issue(n):
        src, nk = pieces[n]
        slot = n % NSLOT
        dst = ring[slot][:, 0:nk, :]
        P.add("pool", lambda e, dst=dst, src=src: e.dma_start(out=dst, in_=src),
              writes=[f"w{slot}"], dma=f"ws{slot}")

    def w_prefetch_upto(n):
        while wstate["issued"] <= n and wstate["issued"] < len(pieces):
            w_issue(wstate["issued"])
            wstate["issued"] += 1

    def w_release(n):
        w_prefetch_upto(n + NSLOT)

    wcur = {"n": 0}

    def w_next():
        n = wcur["n"]
        wcur["n"] += 1
        assert n < wstate["issued"]
        return n, ring[n % NSLOT], f"w{n % NSLOT}"

    w_prefetch_upto(1)

    def memset(eng, ap, val, writes, reads=()):
        P.add(eng, lambda e, ap=ap, val=val: e.memset(ap, val), reads=reads, writes=writes)

    memset("dve", ident_f, 0.0, ["ident_f"])
    P.add("pool", lambda e: e.affine_select(out=ident_f, in_=ident_f, compare_op=ALU.not_equal, fill=1.0,
                                            base=0, pattern=[[-1, 128]], channel_multiplier=1),
          writes=["ident_f"])
    P.add("dve", lambda e: e.tensor_copy(out=ident_b, in_=ident_f), reads=["ident_f"], writes=["ident_b"])
    memset("dve", Bd, 0.0, ["Bd"])
    memset("dve", Bd[0:64, 0:64], 1.0 / 64, ["Bd"])
    memset("dve", Bd[64:128, 64:128], 1.0 / 64, ["Bd"])
    P.add("dve", lambda e: e.tensor_copy(out=Bd_r, in_=Bd), reads=["Bd"], writes=["Bd_r"])
    P.add("dve", lambda e: e.tensor_tensor(out=Cf, in0=ident_f, in1=Bd, op=ALU.subtract),
          reads=["ident_f", "Bd"], writes=["Cf"])
    w_prefetch_upto(NSLOT - 1)
    memset("dve", eps_ap, float(EPS), ["eps"])
    memset("dve", hist_f, 0.0, ["hist_f"])
    memset("dve", zhist, 0.0, ["zhist"])
    memset("dve", ubuf_p[:, :, 0:30], 0.0, [f"uh{c}" for c in range(4)])

    ldn = {"i": 0}

    def sp_dma(out, in_, reads=(), writes=(), pool="ld", npool=16):
        nm = f"{pool}{ldn['i'] % npool}"
        ldn["i"] += 1
        return P.add("sp", lambda e, out=out, in_=in_: e.dma_start(out=out, in_=in_),
                     reads=reads, writes=writes, dma=nm)

    SLOT_BASE = [0, 6, 3]
    PAR = {}

    def pass_tiles(pi):
        p0, sp, samp = PASSES[pi]
        tt = []
        col = 0
        if pi == 0:
            tt.append((SLOT_BASE[pi] % NTT, 0, NMETA, "m"))
            col = NMETA
        while col < sp:
            tt.append(((SLOT_BASE[pi] + len(tt)) % NTT, col, 128, "p"))
            col += 128
        assert col == sp
        if samp:
            tt.append(((SLOT_BASE[pi] + len(tt)) % NTT, sp, 128, "s"))
        for i, tl in enumerate(tt):
            PAR[(pi, tl[0])] = i % 2
        return tt

    def load_tile(pi, tile):
        p0, sp, samp = PASSES[pi]
        (j, col, nt, kind) = tile
        if kind == "s":
            sp_dma(xres[0:128, j, :], xs, writes=[f"xr{j}"])
        elif kind == "m":
            sp_dma(xres[0:NMETA, j, :], meta, writes=[f"xr{j}"])
        else:
            pos = p0 + col
            sp_dma(xres[0:nt, j, :], xp[pos - NMETA:pos - NMETA + nt, :], writes=[f"xr{j}"])

    sp_dma(pstage[0:124, 0, :], wca, reads=[LF], writes=["pstage0"])
    sp_dma(pstage[0:32, 1, :], pmisc, reads=[LF], writes=["pstage1"])
    for k in range(3):
        sp_dma(pstage[0:44, 2 + k, :], wcf[44 * k:44 * (k + 1), :], reads=[LF], writes=[f"pstage{2 + k}"])
    preloaded = set()
    _t0 = pass_tiles(0)
    load_tile(0, _t0[0])
    preloaded.add((0, _t0[0][0]))
    sp_dma(gb[0], bass.AP(g1.tensor, 0, [[0, 128], [1, D]]), writes=["gb0"])
    for tl in _t0[1:]:
        load_tile(0, tl)
        preloaded.add((0, tl[0]))

    def emit_prm():
        pcols = [(0, 124, 0), (1, 32, 124), (2, 44, 156), (3, 44, 200), (4, 44, 244)]

        def f_prm_T(e):
            ins = None
            for (g, rows, col) in pcols:
                ins = e.transpose(out=ps[:, 0, col:col + rows], in_=pstage[0:rows, g, :], identity=ident_f[0:rows, 0:rows])
            return ins
        P.add("pe", f_prm_T, reads=[f"pstage{g}" for g in range(5)] + ["ident_f", LF], writes=["ps0"])
        P.add("act", lambda e: e.activation(out=prm, in_=ps[:, 0, 0:NPRM], func=AF.Copy), reads=["ps0"], writes=["prm"])
        P.add("pe", lambda e: e.matmul(ps[:, 1, 0:4], lhsT=Cf, rhs=prm[:, PBC:PBC + 4], start=True, stop=True),
              reads=["Cf", "prm"], writes=["ps1"])
        P.add("act", lambda e: e.activation(out=cbv, in_=ps[:, 1, 0:4], func=AF.Copy), reads=["ps1"], writes=["cbv"])


    state_groups = []
    for g in range(4):
        sp_dma(sa_tok[0:120, g, :], sa[120 * g:120 * (g + 1), :], reads=[LF], writes=[f"sa_tok{g}"])
    sp_dma(ncas.rearrange("(s t) c -> s t c", t=30)[:, 0:22, :], sa.rearrange("(s t) c -> s t c", t=30)[:, 8:30, :])
    sp_dma(sb_tok[0:32, :], sb_, reads=[LF], writes=["sb_tok"])

    for i, g in enumerate((g1, g2, g3)):
        if i > 0:
            sp_dma(gb[i], bass.AP(g.tensor, 0, [[0, 128], [1, D]]), writes=[f"gb{i}"])

    def sf_load(g):
        sp_dma(sf_tok[g % 2][0:32, :], sf[:, 512 * g:512 * (g + 1)], reads=[LF], writes=[f"sf_tok{g % 2}"])

    def grp_sa(c):
        b = 4 + c % 2

        def f_saT(e):
            ins = None
            for g in range(4):
                ins = e.transpose(out=ps[:, b, 120 * g:120 * (g + 1)], in_=sa_tok[0:120, g, c * 128:(c + 1) * 128],
                                  identity=ident_f[0:120, 0:120])
            return ins
        P.add("pe", f_saT, reads=[f"sa_tok{g}" for g in range(4)] + ["ident_f", LF], writes=[f"ps{b}"])
        P.add("act", lambda e: e.activation(
            out=ubuf_s[:, c, :, 0:30], in_=ps[:, b, 0:480].rearrange("p (s n) -> p s n", s=NSEQ), func=AF.Copy),
            reads=[f"ps{b}"], writes=[f"ush{c}"])

    def grp_sb():
        def f_sbT(e):
            ins = None
            for c in range(4):
                ins = e.transpose(out=ps[:, 6, 32 * c:32 * (c + 1)], in_=sb_tok[0:32, c * 128:(c + 1) * 128],
                                  identity=ident_f[0:32, 0:32])
            return ins
        P.add("pe", f_sbT, reads=["sb_tok", "ident_f", LF], writes=["ps6"])
        P.add("act", lambda e: e.activation(out=zs_hist.rearrange("p c s n -> p (c s n)"), in_=ps[:, 6, 0:128], func=AF.Copy),
              reads=["ps6"], writes=["zs_hist"])

    def grp_sf(g):
        st = sf_tok[g % 2]
        b = 6 + (g + 1) % 2

        def f_sfT(e):
            ins = None
            for c in range(4):
                ins = e.transpose(out=ps[:, b, 32 * c:32 * (c + 1)], in_=st[0:32, c * 128:(c + 1) * 128],
                                  identity=ident_f[0:32, 0:32])
            return ins
        P.add("pe", f_sfT, reads=[f"sf_tok{g % 2}", "ident_f", LF], writes=[f"ps{b}"])
        P.add("act", lambda e: e.activation(
            out=hist_s[:, 4 * g:4 * g + 4, :, :].rearrange("p c s n -> p (c s n)"), in_=ps[:, b, 0:128], func=AF.Copy),
            reads=[f"ps{b}"], writes=["hist_s"])
        if g + 2 < 11:
            sf_load(g + 2)

    for c in range(4):
        state_groups.append(lambda c=c: grp_sa(c))
    state_groups.append(grp_sb)
    for g in range(11):
        state_groups.append(lambda g=g: grp_sf(g))

    fence_n = {"i": 0}

    def fence():
        i = fence_n["i"] % 16
        fence_n["i"] += 1
        P.add("pool", lambda e, i=i: e.memset(scratch[:, i:i + 1], 0.0), writes=[LM, LF])

    def norm_stats(j, nt, gi, bsel):
        ss = stat[:, 2 * j:2 * j + 1]
        rs = stat[:, 2 * j + 1:2 * j + 2]
        h = hn[bsel]
        P.add("act", lambda e: e.activation(out=junk[0:nt, :], in_=xres[0:nt, j, :], func=AF.Square,
                                            accum_out=ss[0:nt, :]),
              reads=[f"xr{j}"], writes=["junk", f"st{j}"])
        P.add("act", lambda e: e.activation(out=ss[0:nt, :], in_=ss[0:nt, :], func=AF.Sqrt, scale=1.0 / D, bias=eps_ap[0:nt, :]),
              reads=[f"st{j}", "eps"], writes=[f"st{j}"])
        P.add("dve", lambda e: e.reciprocal(out=rs[0:nt, :], in_=ss[0:nt, :]),
              reads=[f"st{j}"], writes=[f"rs{j}"])
        P.add("dve", lambda e: e.scalar_tensor_tensor(out=h[0:nt, :], in0=xres[0:nt, j, :], scalar=rs[0:nt, :],
                                                      in1=gb[gi][0:nt, :], op0=ALU.mult, op1=ALU.mult),
              reads=[f"xr{j}", f"rs{j}", f"gb{gi}"], writes=[f"hn{bsel}"])

    def norm_trans(j, nt, col, bsel, b, evac="act"):
        h = hn[bsel]
        pv = psb_bf(b).rearrange("p (k n) -> p k n", k=8)

        def f_T(e):
            ins = None
            for k in range(8):
                ins = e.transpose(out=pv[:, k, 0:nt], in_=h[0:nt, k * 128:(k + 1) * 128], identity=ident_b[0:nt, 0:nt])
            return ins
        P.add("pe", f_T, reads=[f"hn{bsel}", "ident_b"], writes=[f"ps{b}"])
        if evac == "dve":
            P.add("dve", lambda e: e.tensor_copy(out=hT[:, :, col:col + nt], in_=pv[:, :, 0:nt]),
                  reads=[f"ps{b}"], writes=[f"hT{j}"])
        else:
            P.add("act", lambda e: e.activation(out=hT[:, :, col:col + nt], in_=pv[:, :, 0:nt], func=AF.Copy),
                  reads=[f"ps{b}"], writes=[f"hT{j}"])

    def f_mm(e, r, b, off, n0, nw):
        ins = None
        for k in range(8):
            ins = e.matmul(ps[:, b, 0:nw], lhsT=r[:, k, off:off + 128], rhs=hT[:, k, n0:n0 + nw],
                           start=(k == 0), stop=(k == 7))
        return ins

    p1_done = set()
    pend = []
    LAG_S, LAG_T = 2, 4

    def pend_step(newtile=None, tp=None, allow_trans=True):
        for ent in pend:
            ent[1] += 1
        if allow_trans:
            for ent in list(pend):
                tl, tp_ = ent[0], ent[3]
                if ent[2] and ent[1] >= LAG_T and ent[1] - ent[4] >= 3:
                    norm_trans(tl[0], tl[2], tl[1], PAR[(tp_, tl[0])], 4 + PAR[(tp_, tl[0])], evac="dve")
                    pend.remove(ent)
        for ent in pend:
            tl, tp_ = ent[0], ent[3]
            if not ent[2] and ent[1] >= LAG_S:
                if any(o[2] and PAR[(o[3], o[0][0])] == PAR[(tp_, tl[0])] for o in pend if o is not ent):
                    continue
                norm_stats(tl[0], tl[2], 0, PAR[(tp_, tl[0])])
                ent[2] = True
                ent[4] = ent[1]
        if newtile is not None:
            load_tile(tp, newtile)
            p1_done.add((tp, newtile[0]))
            pend.append([newtile, 0, False, tp, 0])
    for pi, (p0, sp, samp) in enumerate(PASSES):
        last_pass = pi == len(PASSES) - 1
        ncols = sp + (128 if samp else 0)
        ttiles = pass_tiles(pi)
        half = sp // 2
        ntiles = [(0, 0, half, "p"), (1, half, sp - half, "p")]
        if samp:
            ntiles.append((2, sp, 128, "s"))

        def tiles_in(n0, nw):
            return [j for (j, col, nt, kind) in ttiles if col < n0 + nw and col + nt > n0]

        def ntiles_of(col, nt):
            return [t for (t, n0, nw, kind) in ntiles if n0 < col + nt and n0 + nw > col]

        todo = [tl for tl in ttiles if (pi, tl[0]) not in p1_done]
        for tl in todo:
            if (pi, tl[0]) not in preloaded:
                load_tile(pi, tl)
        if pi == 0:
            sf_load(0)
            sf_load(1)
        par = lambda j, pi=pi: PAR[(pi, j)]
        for i in range(len(todo) + 2):
            if i >= 2:
                (j, col, nt, kind) = todo[i - 2]
                norm_trans(j, nt, col, par(j), 4 + par(j), evac="dve")
            if i < len(todo):
                (j, col, nt, kind) = todo[i]
                norm_stats(j, nt, 0, par(j))

        if pi == 0:
            emit_prm()
        fence()

        if pi > 0:
            psp = PASSES[pi - 1][1]
            for c in range(4):
                P.add("pool", lambda e, c=c, psp=psp: e.tensor_copy(out=ubuf_p[:, c, 0:30], in_=ubuf_p[:, c, psp:psp + 30]),
                      reads=[f"u{c}_{t}" for t in range(3)] + [LM], writes=[f"uh{c}"])
        dg_todo = []
        for c in range(4):
            for k0 in range(0, 31, 8):
                k1 = min(31, k0 + 8)
                dg_todo.append((c, k0, k1))

        def dg_piece():
            if not dg_todo:
                return
            c, k0, k1 = dg_todo.pop(0)
            nk = k1 - k0
            wcol = prm[:, PA:PA + 124].rearrange("p (k c) -> p k c", c=4)[:, k0:k1, c:c + 1].broadcast_to([128, nk, 128])
            P.add("dve", lambda e: e.tensor_tensor(
                out=dg[:, c * 31 + k0:c * 31 + k1, :], in0=Cf.unsqueeze(1).broadcast_to([128, nk, 128]), in1=wcol, op=ALU.mult),
                reads=["Cf", "prm", LM], writes=[f"dg{c}"])

        job = 0
        wA = {}
        for cp in range(2):
            wA[("g", cp)] = w_next()
            wA[("v", cp)] = w_next()
        if pi == 0:
            order2a = [(tl_, c) for cp_ in range(2) for tl_ in ntiles for c in (2 * cp_, 2 * cp_ + 1)]
        else:
            order2a = [(tl_, c) for tl_ in ntiles for c in range(4)]
        for ((t, n0, nw, kind), c) in order2a:
            if True:
                cp, cc = c // 2, c % 2
                off = cc * 128
                ng, rg, kg = wA[("g", cp)]
                nv, rv, kv = wA[("v", cp)]
                if True:
                    need_ = set(tiles_in(n0, nw))
                    while any(ent[0][0] in need_ for ent in pend):
                        pend_step()
                    bG = job % 2
                    bV = 2 + job % 2
                    sl = job % 2
                    job += 1
                    hkeys = [f"hT{j}" for j in tiles_in(n0, nw)]
                    P.add("pe", lambda e, r=rg, b=bG, off=off, n0=n0, nw=nw: f_mm(e, r, b, off, n0, nw),
                          reads=[kg] + hkeys, writes=[f"ps{bG}"])
                    P.add("pe", lambda e, r=rv, b=bV, off=off, n0=n0, nw=nw: f_mm(e, r, b, off, n0, nw),
                          reads=[kv] + hkeys, writes=[f"ps{bV}"])
                    P.add("act", lambda e, b=bG, sl=sl, nw=nw: e.activation(out=t_sig[sl][:, 0:nw], in_=ps[:, b, 0:nw],
                                                                             func=AF.Sigmoid),
                          reads=[f"ps{bG}", LM], writes=[f"sig{sl}"])
                    if kind == "p":
                        uo = ubuf_p[:, c, 30 + n0:30 + n0 + nw]
                        i0 = ps[:, bV, 0:nw]
                        i1 = t_sig[sl][:, 0:nw]
                    else:
                        uo = ubuf_s[:, c, :, 30:38]
                        i0 = ps[:, bV, 0:128].rearrange("p (s n) -> p s n", s=NSEQ)
                        i1 = t_sig[sl][:, 0:128].rearrange("p (s n) -> p s n", s=NSEQ)
                    P.add("dve", lambda e, uo=uo, i0=i0, i1=i1: e.tensor_tensor(out=uo, in0=i0, in1=i1, op=ALU.mult),
                          reads=[f"ps{bV}", f"sig{sl}"], writes=[f"u{c}_{t}"])
                    dg_piece()
                    for _ in range(2):
                        if state_groups:
                            state_groups.pop(0)()
                    if job >= 2 and pend:
                        pend_step()
                    if last_pass:
                        if kind == "s":
                            P.add("dve", lambda e, c=c, b=bV, sl=sl: e.tensor_tensor(
                                out=u32[:, c, 30:158], in0=ps[:, b, 0:128], in1=t_sig[sl][:, 0:128], op=ALU.mult),
                                reads=[f"ps{bV}", f"sig{sl}"], writes=[f"u32_{c}"])
                        elif t == 1:
                            P.add("dve", lambda e, c=c, b=bV, sl=sl, nw=nw: e.tensor_tensor(
                                out=u32[:, c, 0:30], in0=ps[:, b, nw - 30:nw], in1=t_sig[sl][:, nw - 30:nw], op=ALU.mult),
                                reads=[f"ps{bV}", f"sig{sl}"], writes=[f"u32_{c}"])
        for cp in range(2):
            w_release(wA[("g", cp)][0])
            w_release(wA[("v", cp)][0])
        while pend:
            pend_step()

        bjobs = []
        wB = {}
        for cp in range(2):
            for cc in range(2):
                c = cp * 2 + cc
                for (t, n0, nw, kind) in ntiles:
                    bjobs.append((cp, cc, c, t, n0, nw, kind))

        def b_head(i, samp=samp):
            cp, cc, c, t, n0, nw, kind = bjobs[i]
            off = cc * 128
            zs = c % 2
            if cc == 0 and t == 0:
                wB[cp] = (w_next(), w_next(), w_next())
            wc, wi, wg = wB[cp]
            if t == 0:
                P.add("pool", lambda e: e.tensor_copy(out=zbuf[zs][:, 0:2], in_=zhist[:, c, :]),
                      reads=["zhist", LM], writes=[f"z{zs}"])
                if samp:
                    P.add("pool", lambda e: e.tensor_copy(out=zbuf_s[zs][:, :, 0:2], in_=zs_hist[:, c, :, :]),
                          reads=["zs_hist", LM], writes=[f"zsb{zs}"])
            bC = 4 + i % 2
            bI = 6 + i % 2
            bGt = i % 2
            sl = i % 2
            hkeys = [f"hT{j}" for j in tiles_in(n0, nw)]
            for (w_, b) in ((wc, bC), (wi, bI), (wg, bGt)):
                P.add("pe", lambda e, r=w_[1], b=b: f_mm(e, r, b, off, n0, nw),
                      reads=[w_[2]] + hkeys, writes=[f"ps{b}"])
            P.add("act", lambda e: e.activation(out=t_cg[sl][:, 0:nw], in_=ps[:, bC, 0:nw], func=AF.Copy),
                  reads=[f"ps{bC}", LM], writes=[f"cg{sl}"])
            P.add("act", lambda e: e.activation(out=t_bg[sl][:, 0:nw], in_=ps[:, bGt, 0:nw], func=AF.Copy),
                  reads=[f"ps{bGt}", LM], writes=[f"bg{sl}"])
            if kind == "p":
                zo = zbuf[zs][:, 2 + n0:2 + n0 + nw]
                zi0 = ps[:, bI, 0:nw]
                zi1 = t_cg[sl][:, 0:nw]
                zkey = f"z{zs}"
            else:
                zo = zbuf_s[zs][:, :, 2:10]
                zi0 = ps[:, bI, 0:128].rearrange("p (s n) -> p s n", s=NSEQ)
                zi1 = t_cg[sl][:, 0:128].rearrange("p (s n) -> p s n", s=NSEQ)
                zkey = f"zsb{zs}"
            P.add("dve", lambda e: e.tensor_tensor(out=zo, in0=zi0, in1=zi1, op=ALU.mult),
                  reads=[f"ps{bI}", f"cg{sl}"], writes=[zkey])
            if cc == 1 and t == len(ntiles) - 1:
                for w_ in (wc, wi, wg):
                    w_release(w_[0])

        def b_tail(i, sp=sp, last_pass=last_pass):
            cp, cc, c, t, n0, nw, kind = bjobs[i]
            zs = c % 2
            sl = i % 2
            w0 = prm[:, PWB + 0 * 4 + c:PWB + 0 * 4 + c + 1]
            w1 = prm[:, PWB + 1 * 4 + c:PWB + 1 * 4 + c + 1]
            w2 = prm[:, PWB + 2 * 4 + c:PWB + 2 * 4 + c + 1]
            bb = prm[:, PBB + c:PBB + c + 1]
            if kind == "p":
                taps = [zbuf[zs][:, n0 + k:n0 + k + nw] for k in range(3)]
                ac = accB[sl][:, 0:nw]
                bgv = t_bg[sl][:, 0:nw]
                mo = mixT[:, 4 + c, n0:n0 + nw]
                zkey = f"z{zs}"
            else:
                taps = [zbuf_s[zs][:, :, k:k + 8] for k in range(3)]
                ac = accB[sl][:, 0:128].rearrange("p (s n) -> p s n", s=NSEQ)
                bgv = t_bg[sl][:, 0:128].rearrange("p (s n) -> p s n", s=NSEQ)
                mo = mixT[:, 4 + c, n0:n0 + 128].rearrange("p (s n) -> p s n", s=NSEQ)
                zkey = f"zsb{zs}"
            P.add("act", lambda e: e.activation(out=ac, in_=taps[0], func=AF.Copy, scale=w0),
                  reads=[zkey, "prm", LM], writes=[f"accB{sl}"])
            for (tp, wk) in ((taps[1], w1), (taps[2], w2)):
                P.add("dve", lambda e, tp=tp, wk=wk: e.scalar_tensor_tensor(out=ac, in0=tp, scalar=wk, in1=ac,
                                                                             op0=ALU.mult, op1=ALU.add),
                      reads=[zkey, "prm", f"accB{sl}"], writes=[f"accB{sl}"])
            P.add("dve", lambda e: e.scalar_tensor_tensor(out=mo, in0=ac, scalar=bb, in1=bgv, op0=ALU.mult, op1=ALU.mult),
                  reads=[f"accB{sl}", f"bg{sl}", "prm"], writes=[f"mx{4 + c}_{t}"])
            if t == len(ntiles) - 1:
                if not last_pass:
                    P.add("pool", lambda e: e.tensor_copy(out=zhist[:, c, :], in_=zbuf[zs][:, sp:sp + 2]),
                          reads=[f"z{zs}"], writes=["zhist"])
                else:
                    P.add("pool", lambda e: e.tensor_copy(out=zst[:, c, 0:2], in_=zbuf[zs][:, sp:sp + 2]),
                          reads=[f"z{zs}"], writes=["zst"])
                    P.add("pool", lambda e: e.tensor_copy(
                        out=zst[:, c, 2:34].rearrange("p (s n) -> p s n", s=NSEQ), in_=zbuf_s[zs][:, :, 8:10]),
                        reads=[f"zsb{zs}"], writes=["zst"])

        if state_groups:
            while state_groups:
                state_groups.pop(0)()
        if pi == 0:
            fence()
        nbj = len(bjobs)
        for i in range(nbj + 1):
            if i >= 1:
                b_tail(i - 1)
            if i < nbj:
                b_head(i)
            dg_piece()
        while dg_todo:
            dg_piece()

        jobs = [(c, t, n0, nw, kind) for (t, n0, nw, kind) in ntiles if kind == "p" for c in range(4)]
        if samp:
            sj = [(c, t, n0, nw, kind) for (t, n0, nw, kind) in ntiles if kind == "s" for c in range(4)]
            merged = []
            for i_, jb_ in enumerate(jobs):
                merged.append(jb_)
                if i_ % 2 == 1 and sj:
                    merged.append(sj.pop(0))
            jobs = merged + sj
        last_job_of = {}
        for i_, jb_ in enumerate(jobs):
            last_job_of[jb_[1]] = i_
        RB = [(t_sig[0], "sig0"), (t_sig[1], "sig1"), (t_cg[0], "cg0")]
        nj = len(jobs)

        def s0(i):
            c, t, n0, nw, kind = jobs[i]
            b = 2 + i % 4
            rk = [f"dg{c}", f"u{c}_{t}", (f"u{c}_{t - 1}" if (kind == "p" and t > 0) else (f"uh{c}" if kind == "p" else f"ush{c}"))]

            def f_conv(e):
                ins = None
                for k in range(31):
                    if kind == "p":
                        rhs = ubuf_p[:, c, n0 + k:n0 + k + nw]
                    else:
                        rhs = ubuf_s[:, c, :, k:k + 8]
                    ins = e.matmul(ps[:, b, 0:nw], lhsT=dg[:, c * 31 + k, :], rhs=rhs, start=(k == 0), stop=(k == 30))
                return ins
            P.add("pe", f_conv, reads=rk + [LM], writes=[f"ps{b}"])

        def s0a(i):
            c, t, n0, nw, kind = jobs[i]
            b = 2 + i % 4
            s3_ = i % 3
            P.add("act", lambda e: e.activation(out=t_dsq[s3_][:, 0:nw], in_=ps[:, b, 0:nw], func=AF.Square,
                                                bias=cbv[:, c:c + 1]),
                  reads=[f"ps{b}", "cbv", LM], writes=[f"dsq{s3_}"])

        def s1(i):
            c, t, n0, nw, kind = jobs[i]
            b = 6 + i % 2
            s3_ = i % 3
            P.add("pe", lambda e: e.matmul(ps[:, b, 0:nw], lhsT=Bd_r, rhs=t_dsq[s3_][:, 0:nw],
                                           start=True, stop=True),
                  reads=["Bd_r", f"dsq{s3_}"], writes=[f"ps{b}"])
            rb, rk_ = RB[s3_]
            P.add("act", lambda e: e.activation(out=rb[:, 0:nw], in_=ps[:, b, 0:nw], func=AF.Sqrt, bias=eps_ap),
                  reads=[f"ps{b}", "eps", LM], writes=[rk_])

        def s2(i):
            c, t, n0, nw, kind = jobs[i]
            bD = 2 + i % 4
            sl = i % 2
            s3_ = i % 3
            rb, rk_ = RB[s3_]
            P.add("dve", lambda e: e.reciprocal(out=rb[:, 0:nw], in_=rb[:, 0:nw]),
                  reads=[rk_], writes=[rk_])
            P.add("dve", lambda e: e.scalar_tensor_tensor(out=t_yn[sl][:, 0:nw], in0=ps[:, bD, 0:nw], scalar=cbv[:, c:c + 1],
                                                          in1=rb[:, 0:nw], op0=ALU.add, op1=ALU.mult),
                  reads=[f"ps{bD}", rk_, "cbv", LM], writes=[f"yn{sl}"])

        def s3(i):
            c, t, n0, nw, kind = jobs[i]
            sl = i % 2
            P.add("act", lambda e: e.activation(out=t_yn[sl][:, 0:nw], in_=t_yn[sl][:, 0:nw], func=AF.Silu,
                                                scale=prm[:, PGG + c:PGG + c + 1], bias=prm[:, PGB + c:PGB + c + 1]),
                  reads=[f"yn{sl}", "prm"], writes=[f"yn{sl}"])
            P.add("dve", lambda e: e.tensor_scalar(out=mixT[:, c, n0:n0 + nw], in0=t_yn[sl][:, 0:nw],
                                                   scalar1=prm[:, PBA + c:PBA + c + 1], scalar2=None, op0=ALU.mult),
                  reads=[f"yn{sl}", "prm", LM], writes=[f"mx{c}_{t}"])

        p3 = {"idx": 0, "jo": 0, "wn4": None}
        KORD = (4, 5, 6, 7, 0, 1, 2, 3)

        def p3_tile():
            if p3["wn4"] is None:
                p3["wn4"] = [w_next() for _ in range(4)]
            wn4 = p3["wn4"]
            idx = p3["idx"]
            p3["idx"] += 1
            (j, col, nt, kind) = ttiles[idx]
            mk = [f"mx{c}_{t}" for c in range(8) for t in ntiles_of(col, nt)]
            for q in range(4):
                b = p3["jo"] % 4
                p3["jo"] += 1

                def f_mo(e, r=wn4[q][1], b=b):
                    ins = None
                    for i_, k in enumerate(KORD):
                        ins = e.matmul(ps[0:nt, b, 0:256], lhsT=mixT[:, k, col:col + nt], rhs=r[:, k, :],
                                       start=(i_ == 0), stop=(i_ == 7))
                    return ins
                P.add("pe", f_mo, reads=[wn4[q][2]] + mk + [LM], writes=[f"ps{b}"])
                P.add("dve", lambda e, b=b, q=q: e.tensor_tensor(
                    out=xres[0:nt, j, 256 * q:256 * (q + 1)], in0=ps[0:nt, b, 0:256], in1=xres[0:nt, j, 256 * q:256 * (q + 1)],
                    op=ALU.add),
                    reads=[f"ps{b}", f"xr{j}"], writes=[f"xr{j}"])
            if idx >= 3:
                (j2, col2, nt2, kind2) = ttiles[idx - 3]
                norm_trans(j2, nt2, col2, par(j2), 4 + par(j2))
            if idx >= 1:
                (j2, col2, nt2, kind2) = ttiles[idx - 1]
                norm_stats(j2, nt2, 1, par(j2))

        for it in range(nj + 4):
            for (fn, lag) in ((s0, 0), (s2, 2), (s3, 3), (s1, 1), (s0a, 0)):
                if 0 <= it - lag < nj:
                    fn(it - lag)
            if it >= nj and p3["idx"] < len(ttiles):
                (j_, col_, nt_, kind_) = ttiles[p3["idx"]]
                if all(it >= last_job_of[t_] + 3 for t_ in ntiles_of(col_, nt_)):
                    p3_tile()

        if last_pass:
            def f_uT(e, lo, n, b):
                ins = None
                for c in range(4):
                    ins = e.transpose(out=ps[0:n, b, c * 128:(c + 1) * 128], in_=u32[:, c, lo:lo + n], identity=ident_f)
                return ins
            P.add("pe", lambda e: f_uT(e, 0, 30, 0), reads=[f"u32_{c}" for c in range(4)] + ["ident_f"], writes=["ps0"])
            P.add("act", lambda e: e.activation(out=otokM[0][0:30, :], in_=ps[0:30, 0, :], func=AF.Copy),
                  reads=["ps0"], writes=["dg0", "otokM0"])
            sp_dma(ncap, otokM[0][0:30, :], reads=["otokM0", LM], pool="st")
            P.add("pe", lambda e: f_uT(e, 30, 128, 1), reads=[f"u32_{c}" for c in range(4)] + ["ident_f"], writes=["ps1"])
            P.add("act", lambda e: e.activation(out=otokM[1][:, :], in_=ps[:, 1, :], func=AF.Copy),
                  reads=["ps1"], writes=["dg0", "otokM1"])
            for s in range(NSEQ):
                sp_dma(ncas[30 * s + 22:30 * s + 30, :], otokM[1][8 * s:8 * s + 8, :], reads=["otokM1", LM], pool="st")

            def f_zT(e):
                ins = None
                for c in range(4):
                    ins = e.transpose(out=ps[0:34, 0, c * 128:(c + 1) * 128], in_=zst[:, c, :], identity=ident_f)
                return ins
            P.add("pe", f_zT, reads=["zst", "ident_f"], writes=["ps0"])
            P.add("act", lambda e: e.activation(out=otokM[0][0:34, :], in_=ps[0:34, 0, :], func=AF.Copy),
                  reads=["ps0"], writes=["dg0", "otokM0"])
            sp_dma(ncbp, otokM[0][0:2, :], reads=["otokM0", LM], pool="st")
            sp_dma(ncbs, otokM[0][2:34, :], reads=["otokM0", LM], pool="st")

        while p3["idx"] < len(ttiles):
            p3_tile()
        for w_ in p3["wn4"]:
            w_release(w_[0])
        ntt_ = len(ttiles)
        for idx in range(ntt_, ntt_ + 3):
            if 0 <= idx - 3 < ntt_:
                (j2, col2, nt2, kind2) = ttiles[idx - 3]
                norm_trans(j2, nt2, col2, par(j2), 4 + par(j2))
            if 0 <= idx - 1 < ntt_:
                (j2, col2, nt2, kind2) = ttiles[idx - 1]
                norm_stats(j2, nt2, 1, par(j2))

        fence()

        wP = {}
        bank4 = {"n": 0}

        def p4_head(j, samp=samp):
            if j % 2 == 0:
                wP[j // 2] = (w_next(), w_next())
            wg_, wv_ = wP[j // 2]
            off = (j % 2) * 128
            for hi, (w_, ci) in enumerate(((wg_, j), (wv_, 22 + j))):
                rs_ = (2 * j + hi) % 4
                P.add("pool", lambda e, rs_=rs_, ci=ci: e.tensor_copy(out=raw[rs_][:, 0:2], in_=hist_f[:, ci, :]),
                      reads=["hist_f", LF], writes=[f"rawh{rs_}"])
                if samp:
                    P.add("dve", lambda e, rs_=rs_, ci=ci: e.tensor_copy(out=raw_s[rs_][:, :, 0:2], in_=hist_s[:, ci, :, :]),
                          reads=["hist_s", LF], writes=[f"rawsh{rs_}"])
                for (t, n0, nw, kind) in ntiles:
                    b = bank4["n"] % 6
                    bank4["n"] += 1
                    hkeys = [f"hT{jj}" for jj in tiles_in(n0, nw)]
                    P.add("pe", lambda e, r=w_[1], b=b, n0=n0, nw=nw: f_mm(e, r, b, off, n0, nw),
                          reads=[w_[2]] + hkeys, writes=[f"ps{b}"])
                    if kind == "p":
                        P.add("act", lambda e, rs_=rs_, b=b, n0=n0, nw=nw: e.activation(
                            out=raw[rs_][:, 2 + n0:2 + n0 + nw], in_=ps[:, b, 0:nw], func=AF.Copy),
                            reads=[f"ps{b}", LF], writes=[f"raw{rs_}"])
                    else:
                        P.add("act", lambda e, rs_=rs_, b=b: e.activation(
                            out=raw_s[rs_][:, :, 2:10], in_=ps[:, b, 0:128].rearrange("p (s n) -> p s n", s=NSEQ), func=AF.Copy),
                            reads=[f"ps{b}", LF], writes=[f"raws{rs_}"])
            if j % 2 == 1:
                w_release(wg_[0])
                w_release(wv_[0])

        def p4_mid(j, sp=sp, last_pass=last_pass, samp=samp):
            for hi, ci in enumerate((j, 22 + j)):
                rs_ = (2 * j + hi) % 4
                wk = [prm[:, PWF + k * 44 + ci:PWF + k * 44 + ci + 1] for k in range(3)]
                fw = sp + (2 + NSEQ * 10 if samp else 0)
                rkeys = [f"raw{rs_}", f"rawh{rs_}"] + ([f"raws{rs_}", f"rawsh{rs_}"] if samp else [])
                P.add("act", lambda e, rs_=rs_, wk=wk, fw=fw: e.activation(out=acc[rs_][:, 0:fw], in_=raw[rs_][:, 0:fw],
                                                                            func=AF.Copy, scale=wk[0]),
                      reads=rkeys + ["prm", LF], writes=[f"acc{rs_}"])
                for k in (1, 2):
                    P.add("dve", lambda e, rs_=rs_, wk=wk, k=k, fw=fw: e.scalar_tensor_tensor(
                        out=acc[rs_][:, 0:fw], in0=raw[rs_][:, k:k + fw], scalar=wk[k], in1=acc[rs_][:, 0:fw],
                        op0=ALU.mult, op1=ALU.add),
                        reads=rkeys + ["prm", f"acc{rs_}"], writes=[f"acc{rs_}"])
                if not last_pass:
                    P.add("pool", lambda e, rs_=rs_, ci=ci: e.tensor_copy(out=hist_f[:, ci, :], in_=raw[rs_][:, sp:sp + 2]),
                          reads=[f"raw{rs_}"], writes=["hist_f"])
                else:
                    P.add("pool", lambda e, rs_=rs_, ci=ci: e.tensor_copy(out=stateF[:, ci, 0:2], in_=raw[rs_][:, sp:sp + 2]),
                          reads=[f"raw{rs_}", LF], writes=["stateF"])
                    P.add("act", lambda e, rs_=rs_, ci=ci: e.activation(
                        out=stateF[:, ci, 2:34].rearrange("p (s n) -> p s n", s=NSEQ), in_=raw_s[rs_][:, :, 8:10], func=AF.Copy),
                        reads=[f"raws{rs_}", LF], writes=["stateF"])

        def p4_tail(j, ncols=ncols, samp=samp, sp=sp):
            rg_ = (2 * j) % 4
            rv_ = (2 * j + 1) % 4
            akeys_g = [f"acc{rg_}"]
            akeys_v = [f"acc{rv_}"]
            fw = sp + (2 + NSEQ * 10 if samp else 0)
            P.add("act", lambda e: e.activation(out=acc[rg_][:, 0:fw], in_=acc[rg_][:, 0:fw], func=AF.Silu),
                  reads=akeys_g, writes=akeys_g)
            P.add("dve", lambda e: e.tensor_tensor(out=actT[:, j, 0:sp], in0=acc[rg_][:, 0:sp], in1=acc[rv_][:, 0:sp], op=ALU.mult),
                  reads=akeys_g + akeys_v + [LF], writes=[f"at{j}"])
            if samp:
                sv = lambda a: a[:, sp + 2:sp + 2 + NSEQ * 10].rearrange("p (s n) -> p s n", s=NSEQ)[:, :, 0:8]
                P.add("dve", lambda e: e.tensor_tensor(out=actT[:, j, sp:sp + 128].rearrange("p (s n) -> p s n", s=NSEQ),
                                                       in0=sv(acc[rg_]), in1=sv(acc[rv_]), op=ALU.mult),
                      reads=akeys_g + akeys_v + [LF], writes=[f"at{j}"])

        if samp:
            for rs_ in range(4):
                P.add("pool", lambda e, rs_=rs_: e.memset(raw[rs_][:, 2 + sp + NSEQ * 10:2 + sp + NSEQ * 10 + 4], 0.0),
                      reads=[LF], writes=[f"raws{rs_}"])
        for it in range(22 + 2):
            if 0 <= it - 2 < 22:
                p4_tail(it - 2)
            if 0 <= it - 1 < 22:
                p4_mid(it - 1)
            if it < 22:
                p4_head(it)

        nxt = pass_tiles(pi + 1) if not last_pass else []
        nxt_by_slot = {tl[0]: tl for tl in nxt}
        def next_pass_step(newslot):
            pend_step(nxt_by_slot.get(newslot) if newslot is not None else None, pi + 1)

        cur_slots = set(tl[0] for tl in ttiles)
        tt5 = [tl for tl in ttiles if tl[3] != "m"]
        early = [tl for tl in nxt if tl[0] not in cur_slots]
        for tl in early:
            load_tile(pi + 1, tl)
            p1_done.add((pi + 1, tl[0]))
            pend.append([tl, 0, False, pi + 1, 0])
        jo = 0
        BK5 = [0, 1, 2, 3, 6, 7, 5]

        def md_add(q, wd3, tile, b, k0=0, k1=22, do_add=True):
            (j, col, nt, kind) = tile

            def f_md(e):
                ins = None
                for k in range(k0, k1):
                    ins = e.matmul(ps[0:nt, b, 0:256], lhsT=actT[:, k, col:col + nt], rhs=wd3[k // 8][1][:, k % 8, :],
                                   start=(k == 0), stop=(k == 21))
                return ins
            P.add("pe", f_md, reads=[w_[2] for w_ in wd3] + [f"at{k}" for k in range(k0, k1)] + [LF], writes=[f"ps{b}"])
            if do_add:
                P.add("dve", lambda e: e.tensor_tensor(
                    out=xres[0:nt, j, 256 * q:256 * (q + 1)], in0=ps[0:nt, b, 0:256], in1=xres[0:nt, j, 256 * q:256 * (q + 1)],
                    op=ALU.add),
                    reads=[f"ps{b}", f"xr{j}"], writes=[f"xr{j}"])

        def final_norm(tile, p0=p0):
            (j, col, nt, kind) = tile
            ss = stat[:, 2 * j:2 * j + 1]
            rs = stat[:, 2 * j + 1:2 * j + 2]
            P.add("act", lambda e: e.activation(out=junk[0:nt, :], in_=xres[0:nt, j, :], func=AF.Square, accum_out=ss[0:nt, :]),
                  reads=[f"xr{j}"], writes=["junk", f"st{j}"])
            P.add("act", lambda e: e.activation(out=ss[0:nt, :], in_=ss[0:nt, :], func=AF.Sqrt, scale=1.0 / D, bias=eps_ap[0:nt, :]),
                  reads=[f"st{j}", "eps"], writes=[f"st{j}"])
            P.add("dve", lambda e: e.reciprocal(out=rs[0:nt, :], in_=ss[0:nt, :]),
                  reads=[f"st{j}"], writes=[f"rs{j}"])
            P.add("dve", lambda e: e.scalar_tensor_tensor(
                out=xres[0:nt, j, :], in0=xres[0:nt, j, :], scalar=rs[0:nt, :], in1=gb[2][0:nt, :], op0=ALU.mult, op1=ALU.mult),
                reads=[f"xr{j}", f"rs{j}", "gb2"], writes=[f"xr{j}"])
            if kind == "s":
                sp_dma(ys, xres[0:128, j, :], reads=[f"xr{j}"], pool="st")
            else:
                pos = p0 + col
                sp_dma(yp[pos - NMETA:pos - NMETA + nt, :], xres[0:nt, j, :], reads=[f"xr{j}"], pool="st")
            next_pass_step(j)

        for tl in ttiles:
            if tl[3] == "m":
                next_pass_step(tl[0])
        wd3 = [w_next() for _ in range(3)]
        for (k0, k1) in ((0, 16), (16, 22)):
            for idx, tile in enumerate(tt5):
                md_add(0, wd3, tile, BK5[idx], k0, k1, do_add=(k1 == 22))
                if k1 == 16 and pend:
                    next_pass_step(None)
        for w_ in wd3:
            w_release(w_[0])
        sf_groups = list(range(11)) if last_pass else []

        def sf_out_group():
            if not sf_groups:
                return
            g = sf_groups.pop(0)
            b = 4 + g % 2

            def f_fT(e):
                ins = None
                for c4 in range(4):
                    ins = e.transpose(out=ps[0:34, b, c4 * 128:(c4 + 1) * 128], in_=stateF[:, 4 * g + c4, :], identity=ident_f)
                return ins
            P.add("pe", f_fT, reads=["stateF", "ident_f", LF], writes=[f"ps{b}"])
            P.add("act", lambda e: e.activation(out=otokF[g % 2][0:34, :], in_=ps[0:34, b, :], func=AF.Copy),
                  reads=[f"ps{b}", LF], writes=[f"raw{2 + g % 2}", f"rawh{2 + g % 2}", f"otokF{g % 2}"])
            sp_dma(ncfp[:, 512 * g:512 * (g + 1)], otokF[g % 2][0:2, :], reads=[f"otokF{g % 2}", LF], pool="st")
            sp_dma(ncfs[:, 512 * g:512 * (g + 1)], otokF[g % 2][2:34, :], reads=[f"otokF{g % 2}", LF], pool="st")

        wd3 = [w_next() for _ in range(3)]
        for tile in tt5:
            md_add(1, wd3, tile, jo % 4)
            jo += 1
            if pend:
                next_pass_step(None)
            sf_out_group()
            sf_out_group()
        for w_ in wd3:
            w_release(w_[0])
        wd3a = [w_next() for _ in range(3)]
        wd3b = [w_next() for _ in range(3)]
        for idx, tile in enumerate(tt5):
            md_add(2, wd3a, tile, jo % 4)
            jo += 1
            md_add(3, wd3b, tile, jo % 4)
            jo += 1
            if idx >= 1:
                final_norm(tt5[idx - 1])
        final_norm(tt5[-1])
        for w_ in wd3a + wd3b:
            w_release(w_[0])

    assert wcur["n"] == len(pieces), (wcur["n"], len(pieces))

    dma_names = P.assign()
    from contextlib import ExitStack
    with ExitStack() as ctx:
        semobj = {}
        for eng in Prog.ENGS:
            semobj[("eng", eng)] = ctx.enter_context(nc.semaphore(f"e_{eng}"))
        for nm in dma_names:
            semobj[("dma", nm)] = ctx.enter_context(nc.semaphore(f"d_{nm}"))
        block = ctx.enter_context(nc.Block())

        @block.tensor
        def _(e):
            P.emit_engine("pe", e, semobj)

        @block.scalar
        def _(e):
            P.emit_engine("act", e, semobj)

        @block.vector
        def _(e):
            P.emit_engine("dve", e, semobj)

        @block.gpsimd
        def _(e):
            P.emit_engine("pool", e, semobj)

        @block.sync
        def _(e):
            P.emit_engine("sp", e, semobj, final_waits=True)
    return nc


_NC_CACHE = {}


def _get_nc():
    if "nc" not in _NC_CACHE:
        _NC_CACHE["nc"] = build()
    return _NC_CACHE["nc"]


def make_in_maps(x_prompt, x_sample, state_conv_a, state_conv_b, state_conv_ffn, meta_tokens,
                 norm_mix_g, w_in, w_conv_a, b_conv_a, gn_a_g, gn_a_b, w_conv_b, beta_a, beta_b,
                 w_out, norm_ffn_g, w_up, w_conv_f, w_down, norm_final_g):
    f = lambda a: np.ascontiguousarray(np.asarray(a, dtype=np.float32))
    pmisc = np.concatenate([f(b_conv_a[0]).reshape(4, 128), f(gn_a_g[0]).reshape(4, 128), f(gn_a_b[0]).reshape(4, 128),
                            f(beta_a[0]).reshape(4, 128), f(beta_b[0]).reshape(4, 128),
                            f(w_conv_b[0]).reshape(12, 128)], axis=0)
    shared = {
        "meta": f(meta_tokens), "g1": f(norm_mix_g).reshape(1, D), "g2": f(norm_ffn_g).reshape(1, D),
        "g3": f(norm_final_g).reshape(1, D), "w_in": f(w_in[0]), "w_out": f(w_out[0]), "w_up": f(w_up[0]),
        "w_down": f(w_down[0]), "wca": f(w_conv_a[0]).reshape(124, 128), "pmisc": f(pmisc),
        "wcf": f(w_conv_f[0]).reshape(132, 128),
    }
    maps = []
    for c in range(NCORES):
        m = dict(shared)
        m["xp"] = f(x_prompt[c])
        m["xs"] = f(x_sample[c * NSEQ:(c + 1) * NSEQ]).reshape(NSEQ * LS, D)
        m["sa"] = f(state_conv_a[0, c * NSEQ:(c + 1) * NSEQ]).reshape(NSEQ * 30, AW)
        m["sb"] = f(state_conv_b[0, c * NSEQ:(c + 1) * NSEQ]).reshape(NSEQ * 2, AW)
        m["sf"] = f(state_conv_ffn[0, c * NSEQ:(c + 1) * NSEQ]).reshape(NSEQ * 2, 2 * DFF)
        maps.append(m)
    return maps


def gather(results):
    cat = lambda k: np.stack([r[k] for r in results], axis=0)
    y_prompt = cat("yp")
    y_sample = cat("ys").reshape(NCORES * NSEQ, LS, D)
    ncap = cat("ncap")[None]
    ncbp = cat("ncbp")[None]
    ncfp = cat("ncfp")[None]
    ncas = cat("ncas").reshape(1, NCORES * NSEQ, 30, AW)
    ncbs = cat("ncbs").reshape(1, NCORES * NSEQ, 2, AW)
    ncfs = cat("ncfs").reshape(1, NCORES * NSEQ, 2, 2 * DFF)
    return tuple(np.ascontiguousarray(a, dtype=np.float32) for a in
                 (y_prompt, y_sample, ncap, ncbp, ncfp, ncas, ncbs, ncfs))


def kernel(**inputs):
    nc = _get_nc()
    in_maps = make_in_maps(**inputs)
    res = run_bass_kernel_spmd(nc, in_maps, core_ids=list(range(NCORES)))
    return gather(res.results)
```
